# Optimizing a Trainium2 kernel written in Bass

```python
import jax, jax.numpy as jnp
from jax import lax
import numpy as np

D_MODEL = 2048
BATCH = 8
SEQ = 4096
DEPTH = 2

GRID_W = 64
CTX_LEN = 256
N_BRANCH = 4
WA = 1024
CONV_A = 3
MLA_HEADS = 8
Q_LORA = 768
KV_LORA = 512
NOPE_DIM = 128
ROPE_DIM = 64
V_DIM = 128
QK_DIM = NOPE_DIM + ROPE_DIM
ROPE_THETA = 10000.0
Q_BLOCK = 128
WC = 1024
POOL_WINDOWS = (2, 4, 8, 16)
POOL_GROUP = WC // len(POOL_WINDOWS)
POOL_OUT = D_MODEL // len(POOL_WINDOWS)
WD = 1024
CONV_D = 31
D_FF = 5632
CONV_FF = 3
EPS = 1e-6
LN_EPS = 1e-5

N_A = 3 * WA
N_B = Q_LORA + KV_LORA + ROPE_DIM
N_C = WC
N_D = 2 * WD
N_G = N_BRANCH * D_MODEL
N_IN = N_A + N_B + N_C + N_D + N_G
OFF_Q = N_A
OFF_KV = OFF_Q + Q_LORA
OFF_KR = OFF_KV + KV_LORA
OFF_P = OFF_KR + ROPE_DIM
OFF_D = OFF_P + N_C
OFF_G = OFF_D + N_D
IN_SPLITS = (OFF_Q, OFF_KV, OFF_KR, OFF_P, OFF_D, OFF_G)

kernel_name = 'hybrid_parallel_dit_block'


def rms_norm(x, g):
    xf = x.astype(jnp.float32)
    y = xf * lax.rsqrt(jnp.mean(xf * xf, axis=-1, keepdims=True) + EPS)
    return (y * g.astype(jnp.float32)).astype(x.dtype)


def layer_norm(x, g, b):
    xf = x.astype(jnp.float32)
    mu = jnp.mean(xf, axis=-1, keepdims=True)
    var = jnp.mean(jnp.square(xf - mu), axis=-1, keepdims=True)
    y = (xf - mu) * lax.rsqrt(var + LN_EPS)
    return (y * g.astype(jnp.float32) + b.astype(jnp.float32)).astype(x.dtype)


def modulate(h, shift, scale):
    return h * (1 + scale) + shift


def dwconv(x, w):
    k = w.shape[0]
    return lax.conv_general_dilated(
        x, w[:, None, :].astype(x.dtype), window_strides=(1,),
        padding=[(k // 2, k // 2)], dimension_numbers=('NWC', 'WIO', 'NWC'),
        feature_group_count=x.shape[-1])


def axial_rope_tables(L, dtype):
    rows = L // GRID_W
    row = jnp.broadcast_to(jnp.arange(rows)[:, None], (rows, GRID_W)).reshape(L)
    col = jnp.broadcast_to(jnp.arange(GRID_W)[None, :], (rows, GRID_W)).reshape(L)
    n_freq = ROPE_DIM // 4
    inv = ROPE_THETA ** (-jnp.arange(n_freq, dtype=jnp.float32) / n_freq)
    ang = jnp.stack([row, col], axis=-1).astype(jnp.float32)[:, :, None] * inv
    return jnp.cos(ang).astype(dtype), jnp.sin(ang).astype(dtype)


def apply_rope(x, rope):
    cos, sin = rope
    B, L, H, _ = x.shape
    nf = ROPE_DIM // 4
    r = x[..., NOPE_DIM:].reshape(B, L, H, 2, 2, nf)
    r1, r2 = r[..., 0, :], r[..., 1, :]
    cs, sn = cos[None, :, None], sin[None, :, None]
    rot = jnp.stack([r1 * cs - r2 * sn, r1 * sn + r2 * cs], axis=-2).reshape(B, L, H, ROPE_DIM)
    return jnp.concatenate([x[..., :NOPE_DIM], rot], axis=-1)


def short_conv(u, conv_w, w_out):
    b, cg, hh = jnp.split(u, 3, axis=-1)
    return (b * dwconv(cg * hh, conv_w)) @ w_out


def mla_q(zq, q_norm_g, w_q_up, q_head_g, rope):
    B, L, _ = zq.shape
    q = (rms_norm(zq, q_norm_g) @ w_q_up).reshape(B, L, MLA_HEADS, QK_DIM)
    q = rms_norm(q, q_head_g)
    if rope is not None:
        q = apply_rope(q, rope)
    return q


def mla_kv(zkv, zkr, kv_norm_g, w_kv_up, k_head_g, rope):
    B, L, _ = zkv.shape
    kv = (rms_norm(zkv, kv_norm_g) @ w_kv_up).reshape(B, L, MLA_HEADS, NOPE_DIM + V_DIM)
    k_nope, v = kv[..., :NOPE_DIM], kv[..., NOPE_DIM:]
    k_rope = jnp.broadcast_to(zkr[:, :, None, :], (B, L, MLA_HEADS, ROPE_DIM))
    k = rms_norm(jnp.concatenate([k_nope, k_rope], axis=-1), k_head_g)
    if rope is not None:
        k = apply_rope(k, rope)
    return k, v


def softmax_attention(q, k, v):
    s = jnp.einsum('bqhd,bkhd->bhqk', q, k).astype(jnp.float32) * (QK_DIM ** -0.5)
    p = jax.nn.softmax(s, axis=-1).astype(v.dtype)
    return jnp.einsum('bhqk,bkhd->bqhd', p, v)


def latent_attention(q, k_all, v_all):
    B, L, H, Dq = q.shape
    nb = L // Q_BLOCK
    qb = jnp.moveaxis(q.reshape(B, nb, Q_BLOCK, H, Dq), 1, 0)
    ob = lax.map(lambda qi: softmax_attention(qi, k_all, v_all), qb)
    return jnp.moveaxis(ob, 0, 1).reshape(B, L, H * V_DIM)


def pool_mixer(u, w_pool, pool_scale):
    L = u.shape[1]
    uf = u.astype(jnp.float32)
    cs = jnp.pad(jnp.cumsum(uf, axis=1), ((0, 0), (1, 0), (0, 0)))
    t = jnp.arange(L)
    outs = []
    for g, w in enumerate(POOL_WINDOWS):
        lo = jnp.clip(t - w // 2, 0, L)
        hi = jnp.clip(t - w // 2 + w, 0, L)
        csg = cs[:, :, g * POOL_GROUP:(g + 1) * POOL_GROUP]
        win_sum = jnp.take(csg, hi, axis=1) - jnp.take(csg, lo, axis=1)
        mean = win_sum / (hi - lo).astype(jnp.float32)[None, :, None]
        d = (mean - uf[:, :, g * POOL_GROUP:(g + 1) * POOL_GROUP]).astype(u.dtype)
        outs.append(d @ w_pool[g])
    return jnp.concatenate(outs, axis=-1) * pool_scale


def conformer_conv(u, conv_w, conv_b, ln_g, ln_b, w_out):
    a, gt = jnp.split(u, 2, axis=-1)
    y = a * jax.nn.sigmoid(gt)
    y = dwconv(y, conv_w) + conv_b
    y = jax.nn.silu(layer_norm(y, ln_g, ln_b))
    return y @ w_out


def gated_merge(zg, ys, w_out):
    gates = jax.nn.sigmoid(zg)
    merged = gates[..., :D_MODEL] * ys[0]
    for i in range(1, N_BRANCH):
        merged = merged + gates[..., i * D_MODEL:(i + 1) * D_MODEL] * ys[i]
    return merged @ w_out


def conv_ffn(h, w_up, conv_w, conv_b, w_down):
    gate, val = jnp.split(h @ w_up, 2, axis=-1)
    gate = dwconv(gate, conv_w) + conv_b
    return (jax.nn.silu(gate) * val) @ w_down


def setup_inputs(seed: int = 0) -> dict:
    key = jax.random.key(seed)
    ks = iter(jax.random.split(key, 40))
    D = D_MODEL

    def nrm(shape, scale):
        return jax.random.normal(next(ks), shape, jnp.float32) * scale

    def gain(shape):
        return 1.0 + 0.02 * jax.random.normal(next(ks), shape, jnp.float32)

    return {
        'x': nrm((BATCH, SEQ, D), 1.0),
        'c': nrm((BATCH, D), 1.0),
        'ctx': nrm((BATCH, CTX_LEN, D), 1.0),
        'c_ctx': nrm((D,), 1.0),
        'ada_w': nrm((DEPTH, D, 6 * D), 0.5 * D ** -0.5),
        'ada_b': nrm((DEPTH, 6 * D), 0.02),
        'norm1_g': gain((DEPTH, D)),
        'w_in': nrm((DEPTH, D, N_IN), D ** -0.5),
        'conv_a_w': nrm((DEPTH, CONV_A, WA), CONV_A ** -0.5),
        'w_a_out': nrm((DEPTH, WA, D), WA ** -0.5),
        'q_norm_g': gain((DEPTH, Q_LORA)),
        'w_q_up': nrm((DEPTH, Q_LORA, MLA_HEADS * QK_DIM), Q_LORA ** -0.5),
        'kv_norm_g': gain((DEPTH, KV_LORA)),
        'w_kv_up': nrm((DEPTH, KV_LORA, MLA_HEADS * (NOPE_DIM + V_DIM)), KV_LORA ** -0.5),
        'q_head_g': gain((DEPTH, QK_DIM)),
        'k_head_g': gain((DEPTH, QK_DIM)),
        'w_mla_out': nrm((DEPTH, MLA_HEADS * V_DIM, D), (MLA_HEADS * V_DIM) ** -0.5),
        'w_pool': nrm((DEPTH, len(POOL_WINDOWS), POOL_GROUP, POOL_OUT), POOL_GROUP ** -0.5),
        'pool_scale': gain((DEPTH, D)),
        'conv_d_w': nrm((DEPTH, CONV_D, WD), CONV_D ** -0.5),
        'conv_d_b': nrm((DEPTH, WD), 0.02),
        'cd_ln_g': gain((DEPTH, WD)),
        'cd_ln_b': nrm((DEPTH, WD), 0.02),
        'w_d_out': nrm((DEPTH, WD, D), WD ** -0.5),
        'w_out': nrm((DEPTH, D, D), D ** -0.5),
        'norm2_g': gain((DEPTH, D)),
        'w_up': nrm((DEPTH, D, 2 * D_FF), D ** -0.5),
        'conv_ff_w': nrm((DEPTH, CONV_FF, D_FF), CONV_FF ** -0.5),
        'conv_ff_b': nrm((DEPTH, D_FF), 0.02),
        'w_down': nrm((DEPTH, D_FF, D), D_FF ** -0.5),
    }


def reference(x, c, ctx, c_ctx, ada_w, ada_b, norm1_g, w_in, conv_a_w, w_a_out, q_norm_g, w_q_up,
              kv_norm_g, w_kv_up, q_head_g, k_head_g, w_mla_out, w_pool, pool_scale, conv_d_w,
              conv_d_b, cd_ln_g, cd_ln_b, w_d_out, w_out, norm2_g, w_up, conv_ff_w, conv_ff_b, w_down):
    B, L, _ = x.shape
    n_ctx = ctx.shape[1]
    rope = axial_rope_tables(L, x.dtype)
    for l in range(DEPTH):
        last = l == DEPTH - 1
        mod_x = (jax.nn.silu(c) @ ada_w[l] + ada_b[l])[:, None, :]
        mod_c = (jax.nn.silu(c_ctx) @ ada_w[l] + ada_b[l])[None, None, :]
        sh1, sc1, g1, sh2, sc2, g2 = jnp.split(mod_x, 6, axis=-1)
        csh1, csc1, cg1, csh2, csc2, cg2 = jnp.split(mod_c, 6, axis=-1)

        hx = modulate(rms_norm(x, norm1_g[l]), sh1, sc1)
        hc = modulate(rms_norm(ctx, norm1_g[l]), csh1, csc1)
        zx = jnp.split(hx @ w_in[l], IN_SPLITS, axis=-1)
        if last:
            zkvr = hc @ w_in[l][:, OFF_KV:OFF_P]
            zc_kv, zc_kr = zkvr[..., :KV_LORA], zkvr[..., KV_LORA:]
        else:
            zc = jnp.split(hc @ w_in[l], IN_SPLITS, axis=-1)
            zc_kv, zc_kr = zc[2], zc[3]

        k_c, v_c = mla_kv(zc_kv, zc_kr, kv_norm_g[l], w_kv_up[l], k_head_g[l], None)
        k_x, v_x = mla_kv(zx[2], zx[3], kv_norm_g[l], w_kv_up[l], k_head_g[l], rope)
        q_x = mla_q(zx[1], q_norm_g[l], w_q_up[l], q_head_g[l], rope)
        k_all = jnp.concatenate([k_c, k_x], axis=1)
        v_all = jnp.concatenate([v_c, v_x], axis=1)
        att_x = latent_attention(q_x, k_all, v_all) @ w_mla_out[l]

        ys_x = (short_conv(zx[0], conv_a_w[l], w_a_out[l]),
                att_x,
                pool_mixer(zx[4], w_pool[l], pool_scale[l]),
                conformer_conv(zx[5], conv_d_w[l], conv_d_b[l], cd_ln_g[l], cd_ln_b[l], w_d_out[l]))
        x_new = x + g1 * gated_merge(zx[6], ys_x, w_out[l])
        hx2 = modulate(rms_norm(x_new, norm2_g[l]), sh2, sc2)
        x_new = x_new + g2 * conv_ffn(hx2, w_up[l], conv_ff_w[l], conv_ff_b[l], w_down[l])

        if not last:
            q_c = mla_q(zc[1], q_norm_g[l], w_q_up[l], q_head_g[l], None)
            att_c = softmax_attention(q_c, k_c, v_c).reshape(B, n_ctx, MLA_HEADS * V_DIM) @ w_mla_out[l]
            ys_c = (short_conv(zc[0], conv_a_w[l], w_a_out[l]),
                    att_c,
                    pool_mixer(zc[4], w_pool[l], pool_scale[l]),
                    conformer_conv(zc[5], conv_d_w[l], conv_d_b[l], cd_ln_g[l], cd_ln_b[l], w_d_out[l]))
            ctx = ctx + cg1 * gated_merge(zc[6], ys_c, w_out[l])
            hc2 = modulate(rms_norm(ctx, norm2_g[l]), csh2, csc2)
            ctx = ctx + cg2 * conv_ffn(hc2, w_up[l], conv_ff_w[l], conv_ff_b[l], w_down[l])
        x = x_new
    return x
```

```python
import math
from contextlib import ExitStack

import numpy as np
import concourse.bass as bass
import concourse.mybir as mybir
from concourse.bass_utils import run_bass_kernel_spmd

F32 = mybir.dt.float32
BF16 = mybir.dt.bfloat16
AF = mybir.ActivationFunctionType
ALU = mybir.AluOpType

D = 2048
KC = 16
NCTX = 256
NX = 4096
TA = NCTX + NX
NTT = TA // 128
DEPTH = 2
BLOCKS = [(0, 256)] + [(256 + 512 * i, 512) for i in range(8)]
EPS = 1e-6
LN_EPS = 1e-5
ZA, ZQ, ZKV, ZP, ZDA, ZDG, ZG, ZKR = 0, 3072, 3840, 4352, 5376, 6400, 7424, 15616
NIN = 15680
DFF = 5632
NFC = DFF // 128
SEM_LIMIT = 30000
PHASE_MARKS = []

SP_CA = 0
SP_CD = SP_CA + 24
SP_CDB = SP_CD + 248
SP_LNG = SP_CDB + 8
SP_LNB = SP_LNG + 8
SP_PS = SP_LNB + 8
SP_CF = SP_PS + 16
SP_CFB = SP_CF + 132
SP_QN = SP_CFB + 44
SP_KVN = SP_QN + 6
SP_QHN = SP_KVN + 4
SP_QHR = SP_QHN + 1
SP_KHN = SP_QHR + 1
SP_KHR = SP_KHN + 1
NSMALL = SP_KHR + 1


class Slot:
    __slots__ = ("w", "r", "ds", "psum")

    def __init__(self):
        self.w = None
        self.r = {}
        self.ds = None
        self.psum = False


class DSem:
    __slots__ = ("h", "cnt", "sw")

    def __init__(self, h):
        self.h = h
        self.cnt = 0
        self.sw = False


class Eng:
    def __init__(self, kb, name, h, compute=True):
        self.kb = kb
        self.name = name
        self.h = h
        self.sem = kb.nc.alloc_semaphore("e_" + name)
        self.cnt = 0
        self.seen = {}
        self.pend = False


class Tile:
    def __init__(self, t, nslots=1):
        self.t = t
        self.sl = [Slot() for _ in range(nslots)]

    @property
    def s(self):
        return self.sl[0]


class KB:
    def __init__(self, nc):
        self.nc = nc
        self.pe = Eng(self, "pe", nc.tensor)
        self.act = Eng(self, "act", nc.scalar)
        self.dve = Eng(self, "dve", nc.vector)
        self.pool = Eng(self, "pool", nc.gpsimd)
        self.sp = Eng(self, "sp", nc.sync)
        self.engs = [self.pe, self.act, self.dve, self.pool, self.sp]
        self.free_ds = []
        self.free_ds_sw = []
        self.live_ds = []
        self.nsem = 5
        self.tid = 0

    def tile(self, st, shape, dt, nslots=1, name=None):
        self.tid += 1
        t = st.enter_context(self.nc.sbuf_tensor(f"{name or 't'}_{self.tid}", list(shape), dt))
        return Tile(t, nslots)

    def get_ds(self, sw):
        fl = self.free_ds_sw if sw else self.free_ds
        if fl:
            ds = fl.pop()
        else:
            self.nsem += 1
            ds = DSem(self.nc.alloc_semaphore(f"d{self.nsem}"))
            ds.sw = sw
        self.live_ds.append(ds)
        return ds

    def _need(self, need, sv):
        if sv is None:
            return
        sem, val = sv
        if need.get(sem, 0) < val:
            need[sem] = val

    def _waits(self, eng, reads, writes):
        need = {}
        for s in reads:
            self._need(need, s.w)
            if s.psum:
                for sem, val in s.r.items():
                    if sem is not eng.sem:
                        need[sem] = max(need.get(sem, 0), val)
        for s in writes:
            self._need(need, s.w)
            for sem, val in s.r.items():
                need[sem] = max(need.get(sem, 0), val)
        for sem, val in need.items():
            if sem is eng.sem and eng is self.pe:
                continue
            if eng.seen.get(sem, 0) < val:
                eng.h.wait_ge(sem, val)
                eng.seen[sem] = val

    def op(self, eng, fn, reads=(), writes=(), inc=True):
        self._waits(eng, reads, writes)
        ins = fn(eng.h)
        val = eng.cnt + 1
        sem = eng.sem
        for s in reads:
            if s.r.get(sem, 0) < val:
                s.r[sem] = val
        for s in writes:
            s.w = (sem, val)
            s.r = {}
        if inc:
            ins.then_inc(sem, 1)
            eng.cnt = val
            eng.pend = False
            if eng.cnt >= SEM_LIMIT:
                self.nsem += 1
                eng.sem = self.nc.alloc_semaphore(f"e_{eng.name}_{self.nsem}")
                eng.cnt = 0
        else:
            eng.pend = True
        return ins

    def dma(self, q, out, in_, reads=(), writes=()):
        self._waits(q, reads, writes)
        owner = writes[0] if writes else reads[0]
        if owner.ds is None:
            owner.ds = {}
        sw = q is self.pool
        if sw not in owner.ds or owner.ds[sw].cnt + 16 > SEM_LIMIT:
            owner.ds[sw] = self.get_ds(sw)
        ds = owner.ds[sw]
        ds.cnt += 16
        q.h.dma_start(out=out, in_=in_).then_inc(ds.h, 16)
        for s in reads:
            if s.r.get(ds.h, 0) < ds.cnt:
                s.r[ds.h] = ds.cnt
        for s in writes:
            s.w = (ds.h, ds.cnt)
            s.r = {}

    def barrier(self):
        assert not self.pe.pend
        PHASE_MARKS.append(self.nc.n_instructions())
        for e in self.engs:
            for f in self.engs:
                if f is not e and f.cnt > 0 and e.seen.get(f.sem, 0) < f.cnt:
                    e.h.wait_ge(f.sem, f.cnt)
                    e.seen[f.sem] = f.cnt
            for ds in self.live_ds:
                if ds.cnt > 0 and e.seen.get(ds.h, 0) < ds.cnt:
                    e.h.wait_ge(ds.h, ds.cnt)
                    e.seen[ds.h] = ds.cnt
        for ds in self.live_ds:
            if ds.cnt + 2048 < SEM_LIMIT:
                (self.free_ds_sw if ds.sw else self.free_ds).append(ds)
        self.live_ds = []

    def mm(self, out, lhsT, rhs, start, stop, reads, writes, inc):
        return self.op(self.pe, lambda e: e.matmul(out, lhsT, rhs, start=start, stop=stop), reads, writes, inc)

    def actf(self, out, in_, func, reads, writes, bias=None, scale=None, accum_out=None):
        kw = {}
        if bias is not None:
            kw["bias"] = bias
        if scale is not None:
            kw["scale"] = scale
        if accum_out is not None:
            kw["accum_out"] = accum_out
        return self.op(self.act, lambda e: e.activation(out=out, in_=in_, func=func, **kw), reads, writes)

    def tt(self, eng, out, in0, in1, op, reads, writes):
        return self.op(eng, lambda e: e.tensor_tensor(out=out, in0=in0, in1=in1, op=op), reads, writes)

    def ts(self, eng, out, in0, s1, s2, op0, op1, reads, writes):
        if s2 is None:
            return self.op(eng, lambda e: e.tensor_scalar(out=out, in0=in0, scalar1=s1, scalar2=None, op0=op0),
                           reads, writes)
        return self.op(eng, lambda e: e.tensor_scalar(out=out, in0=in0, scalar1=s1, scalar2=s2, op0=op0, op1=op1),
                       reads, writes)

    def stt(self, eng, out, in0, scalar, in1, op0, op1, reads, writes):
        return self.op(eng, lambda e: e.scalar_tensor_tensor(out=out, in0=in0, scalar=scalar, in1=in1,
                                                             op0=op0, op1=op1), reads, writes)

    def copy(self, eng, out, in_, reads, writes):
        if eng is self.act:
            return self.actf(out, in_, AF.Copy, reads, writes)
        return self.op(eng, lambda e: e.tensor_copy(out=out, in_=in_), reads, writes)

    def memset(self, eng, ap, val, writes):
        return self.op(eng, lambda e: e.memset(ap, val), (), writes)


def poff(t, P):
    return P + t if t < NCTX else 2 * P + t


def build(n_layers=DEPTH, dbg=False, stop_after=None):
    nc = bass.Bass("TRN2", target_bir_lowering=False)
    kb = KB(nc)
    pe, act, dve, pool, sp = kb.pe, kb.act, kb.dve, kb.pool, kb.sp
    dbg = set(dbg) if dbg else set()

    def din(name, shape, dt=F32):
        return nc.dram_tensor(name, list(shape), dt, kind="ExternalInput").ap()

    def dscr(name, shape, dt=BF16):
        return nc.dram_tensor(name, list(shape), dt, kind=("ExternalOutput" if name in dbg else "Internal")).ap()

    xin = din("xin", [TA, D])
    cc = din("cc", [128, KC, 2])
    ada_w = din("ada_w", [DEPTH, D, 6 * D])
    ada_b = din("ada_b", [DEPTH, 6 * D])
    norm1_g = din("norm1_g", [DEPTH, D])
    norm2_g = din("norm2_g", [DEPTH, D])
    w_in = din("w_in", [DEPTH, D, NIN])
    w_aout = din("w_aout", [DEPTH, 3, 1024, D])
    w_pool = din("w_pool", [DEPTH, 1024, 512])
    w_out = din("w_out", [DEPTH, D, D])
    w_qup = din("w_qup", [DEPTH, 768, 1536])
    w_kvup = din("w_kvup", [DEPTH, 512, 2048])
    w_up = din("w_up", [DEPTH, D, 2 * DFF])
    w_down = din("w_down", [DEPTH, DFF, D])
    small = din("small", [DEPTH, 128, NSMALL])
    rope_c = din("rope_c", [128, NX])
    rope_s = din("rope_s", [128, NX])
    pmat = din("pmat", [128, 128])
    ident_in = din("ident", [128, 128])
    out = nc.dram_tensor("out", [NX, D], F32, kind="ExternalOutput").ap()

    modd = dscr("modd", [2, 6 * D], F32)
    zx = dscr("zx", [NIN, TA])
    qT = dscr("qT", [1536, TA])
    kT = dscr("kT", [1536, TA])
    vv = dscr("vv", [TA, 1024])
    aT = dscr("aT", [4, 1024, TA])
    xmid = dscr("xmid", [TA, D], F32)
    xl0 = dscr("xl0", [TA, D], F32)
    ffT = dscr("ffT", [2 * DFF, TA])
    fT = dscr("fT", [DFF, TA])

    psum = [Tile(nc.alloc_psum_tensor(f"ps{i}", [128, 512], F32)) for i in range(8)]
    for p_ in psum:
        p_.s.psum = True
    psn = [0]

    def next_ps():
        p = psum[psn[0] % 8]
        psn[0] += 1
        return p

    with ExitStack() as gst, nc.Block() as block:
        ident = kb.tile(gst, [128, 128], BF16, name="ident")
        ones = kb.tile(gst, [128, 128], BF16, name="ones")
        pm = kb.tile(gst, [128, 128], BF16, name="pm")
        ctmp = kb.tile(gst, [128, 128], F32, name="ctmp")
        kb.dma(sp, ctmp.t[:, :], ident_in, (), [ctmp.s])
        kb.copy(dve, ident.t[:, :], ctmp.t[:, :], [ctmp.s], [ident.s])
        kb.dma(sp, ctmp.t[:, :], pmat, (), [ctmp.s])
        kb.copy(dve, pm.t[:, :], ctmp.t[:, :], [ctmp.s], [pm.s])
        kb.memset(dve, ones.t[:, :], 1.0, [ones.s])
        epsb = kb.tile(gst, [128, 1], F32, name="epsb")
        lnepsb = kb.tile(gst, [128, 1], F32, name="lnepsb")
        kb.memset(dve, epsb.t[:, :], EPS, [epsb.s])
        kb.memset(dve, lnepsb.t[:, :], LN_EPS, [lnepsb.s])

        def phase_done(name):
            kb.barrier()
            return stop_after == name

        def p0_mod(l):
            with ExitStack() as st:
                cs = kb.tile(st, [128, KC, 2], F32)
                cb = kb.tile(st, [128, KC, 2], BF16)
                ab = kb.tile(st, [2, 6 * D], F32)
                mo = kb.tile(st, [2, 6 * D], F32)
                wt = [kb.tile(st, [128, KC, 512], BF16) for _ in range(3)]
                kb.dma(sp, cs.t[:, :, :], cc, (), [cs.s])
                kb.actf(cb.t[:, :, :], cs.t[:, :, :], AF.Silu, [cs.s], [cb.s])
                kb.dma(sp, ab.t[:, :], ada_b[l:l + 1, :].partition_broadcast(2), (), [ab.s])
                wv = ada_w[l].rearrange("(kc p) n -> p kc n", p=128)
                for g in range(24):
                    w = wt[g % 3]
                    kb.dma(pool, w.t[:, :, :], wv[:, :, g * 512:(g + 1) * 512], (), [w.s])
                    ps = next_ps()
                    for kc in range(KC):
                        kb.mm(ps.t[0:2, :], cb.t[:, kc, :], w.t[:, kc, :], kc == 0, kc == KC - 1,
                              [cb.s, w.s], [ps.s], kc == KC - 1)
                    kb.tt(dve, mo.t[:, g * 512:(g + 1) * 512], ps.t[0:2, :], ab.t[:, g * 512:(g + 1) * 512],
                          ALU.add, [ps.s, ab.s], [mo.s])
                kb.dma(sp, modd, mo.t[:, :], [mo.s], ())
                kb.barrier()

        def make_hT(st_outer, l, src, norm_g, sh_off, sc_off, seqs):
            hT = kb.tile(st_outer, [128, KC, TA], BF16, nslots=NTT, name="hT")
            with ExitStack() as st:
                xt = [kb.tile(st, [128, D], F32) for _ in range(3)]
                hb = [kb.tile(st, [128, D], BF16) for _ in range(3)]
                gsc = kb.tile(st, [128, D], F32)
                gtm = kb.tile(st, [128, D], F32)
                shb = kb.tile(st, [128, D], F32)
                ss = [kb.tile(st, [128, 1], F32) for _ in range(3)]
                rs = [kb.tile(st, [128, 1], F32) for _ in range(3)]
                cur = [None]
                tiles = [tt for tt in range(NTT) if tt >= 2 or "c" in seqs]

                def stage_a(tt):
                    which = 1 if tt < 2 else 0
                    if which != cur[0]:
                        cur[0] = which
                        kb.dma(sp, gsc.t[:, :], modd[which:which + 1, sc_off:sc_off + D].partition_broadcast(128),
                               (), [gsc.s])
                        kb.dma(sp, gtm.t[:, :], norm_g[l:l + 1, :].partition_broadcast(128), (), [gtm.s])
                        kb.dma(sp, shb.t[:, :], modd[which:which + 1, sh_off:sh_off + D].partition_broadcast(128),
                               (), [shb.s])
                        kb.stt(dve, gsc.t[:, :], gsc.t[:, :], 1.0, gtm.t[:, :], ALU.add, ALU.mult,
                               [gsc.s, gtm.s], [gsc.s])
                    x = xt[tt % 3]
                    h = hb[tt % 3]
                    s_ = ss[tt % 3]
                    r_ = rs[tt % 3]
                    t0 = tt * 128
                    kb.dma(sp, x.t[:, :], src[t0:t0 + 128, :], (), [x.s])
                    kb.memset(dve, s_.t[:, :], 0.0, [s_.s])
                    kb.actf(h.t[:, :], x.t[:, :], AF.Square, [x.s], [h.s, s_.s], accum_out=s_.t[:, 0:1])
                    kb.actf(r_.t[:, :], s_.t[:, :], AF.Sqrt, [s_.s], [r_.s], bias=epsb.t[:, 0:1], scale=1.0 / D)
                    kb.op(dve, lambda e, r_=r_: e.reciprocal(out=r_.t[:, :], in_=r_.t[:, :]), [r_.s], [r_.s])
                    kb.stt(dve, x.t[:, :], x.t[:, :], r_.t[:, 0:1], gsc.t[:, :], ALU.mult, ALU.mult,
                           [x.s, r_.s, gsc.s], [x.s])
                    kb.tt(pool, h.t[:, :], x.t[:, :], shb.t[:, :], ALU.add, [x.s, shb.s], [h.s])

                def stage_b(tt):
                    h = hb[tt % 3]
                    t0 = tt * 128
                    pa = next_ps()
                    pb = next_ps()
                    for half, p in enumerate((pa, pb)):
                        pv = p.t[:, :].bitcast(BF16)
                        for j in range(8):
                            kcx = half * 8 + j
                            kb.op(pe, lambda e, pv=pv, j=j, kcx=kcx: e.transpose(
                                pv[:, j * 128:(j + 1) * 128], h.t[:, kcx * 128:(kcx + 1) * 128], ident.t[:, :]),
                                [h.s, ident.s], [p.s], inc=(j == 7))
                        eng = act if half == 0 else dve
                        kb.copy(eng, hT.t[:, half * 8:half * 8 + 8, t0:t0 + 128],
                                pv[:, 0:1024].rearrange("p (k t) -> p k t", k=8), [p.s], [hT.sl[tt]])

                for i, tt in enumerate(tiles):
                    if i == 0:
                        stage_a(tt)
                    if i + 1 < len(tiles):
                        stage_a(tiles[i + 1])
                    stage_b(tt)
                kb.barrier()
            return hT

        def hT_slots(hT, t0, n):
            return [hT.sl[i] for i in range(t0 // 128, (t0 + n) // 128)]

        def proj_fm(l, hT, wsrc, ncols, dst, seqs, func_of_chunk, ctx_chunks=None):
            with ExitStack() as st:
                wt = [kb.tile(st, [128, KC, 512], BF16) for _ in range(2)]
                orow = [kb.tile(st, [128, TA], BF16, nslots=len(BLOCKS)) for _ in range(3)]
                wv = wsrc.rearrange("(kc p) n -> p kc n", p=128)
                ngroups = (ncols + 511) // 512
                ci = 0
                for g in range(ngroups):
                    c0 = g * 512
                    gw = min(512, ncols - c0)
                    w = wt[g % 2]
                    kb.dma(pool, w.t[:, :, 0:gw], wv[:, :, c0:c0 + gw], (), [w.s])
                    for cj in range((gw + 127) // 128):
                        m = min(128, gw - cj * 128)
                        chunk = (c0 + cj * 128) // 128
                        func = func_of_chunk(chunk)
                        orw = orow[ci % 3]
                        ci += 1
                        used = []
                        for bi, (t0, n) in enumerate(BLOCKS):
                            if bi == 0:
                                if "c" not in seqs and (ctx_chunks is None or chunk not in ctx_chunks):
                                    continue
                            used.append(bi)
                            ps = next_ps()
                            hs = hT_slots(hT, t0, n)
                            for kc in range(KC):
                                kb.mm(ps.t[0:m, 0:n], w.t[:, kc, cj * 128:cj * 128 + m], hT.t[:, kc, t0:t0 + n],
                                      kc == 0, kc == KC - 1, [w.s] + hs, [ps.s], kc == KC - 1)
                            if func is None:
                                eng = dve if (bi % 2 == 0) else act
                                kb.copy(eng, orw.t[0:m, t0:t0 + n], ps.t[0:m, 0:n], [ps.s], [orw.sl[bi]])
                            else:
                                kb.actf(orw.t[0:m, t0:t0 + n], ps.t[0:m, 0:n], func, [ps.s], [orw.sl[bi]])
                        ta = BLOCKS[used[0]][0]
                        r0 = c0 + cj * 128
                        kb.dma(sp, dst[r0:r0 + m, ta:TA], orw.t[0:m, ta:TA], [orw.sl[b] for b in used], ())
                kb.barrier()

        def load_small(st, l):
            sm = kb.tile(st, [128, NSMALL], F32, name="small")
            kb.dma(sp, sm.t[:, :], small[l], (), [sm.s])
            return sm

        def row_load(q, dstt, P, src_rows, seqs, slot):
            if "c" in seqs:
                kb.dma(q, dstt.t[:, P:P + NCTX], src_rows[:, 0:NCTX], (), [slot])
            kb.dma(q, dstt.t[:, 2 * P + NCTX:2 * P + TA], src_rows[:, NCTX:TA], (), [slot])

        def make_diags(eng, dg, sm, col0, ntap, stride=1):
            for k in range(ntap):
                kb.ts(eng, dg.t[:, k, :], ident.t[:, :], sm.t[:, col0 + k * stride:col0 + k * stride + 1], None,
                      ALU.mult, None, [ident.s, sm.s], [dg.s])

        def seq_blocks(seqs):
            return [(bi, t0, n) for bi, (t0, n) in enumerate(BLOCKS) if bi > 0 or "c" in seqs]

        def p2a_shortconv(l, seqs):
            P = 1
            W = TA + 3 * P
            with ExitStack() as st:
                sm = load_small(st, l)
                brow = [kb.tile(st, [128, TA], BF16) for _ in range(2)]
                crow = [kb.tile(st, [128, W], BF16) for _ in range(2)]
                hrow = [kb.tile(st, [128, W], BF16) for _ in range(2)]
                orow = [kb.tile(st, [128, TA], BF16, nslots=len(BLOCKS)) for _ in range(2)]
                dgs = [kb.tile(st, [128, 3, 128], BF16) for _ in range(2)]
                for t in crow + hrow:
                    kb.memset(pool, t.t[:, :], 0.0, [t.s])
                ta = 0 if "c" in seqs else NCTX
                for c in range(8):
                    b, cg, hh, orw, dg = brow[c % 2], crow[c % 2], hrow[c % 2], orow[c % 2], dgs[c % 2]
                    kb.dma(sp, b.t[:, ta:TA], zx[ZA + c * 128:ZA + (c + 1) * 128, ta:TA], (), [b.s])
                    row_load(sp, cg, P, zx[ZA + 1024 + c * 128:ZA + 1024 + (c + 1) * 128, :], seqs, cg.s)
                    row_load(sp, hh, P, zx[ZA + 2048 + c * 128:ZA + 2048 + (c + 1) * 128, :], seqs, hh.s)
                    kb.tt(pool, cg.t[:, :], cg.t[:, :], hh.t[:, :], ALU.mult, [cg.s, hh.s], [cg.s])
                    make_diags(dve, dg, sm, SP_CA + c * 3, 3)
                    for bi, t0, n in seq_blocks(seqs):
                        ps = next_ps()
                        p0 = poff(t0, P) - P
                        for k in range(3):
                            kb.mm(ps.t[:, 0:n], dg.t[:, k, :], cg.t[:, p0 + k:p0 + k + n], k == 0, k == 2,
                                  [dg.s, cg.s], [ps.s], k == 2)
                        kb.tt(dve, orw.t[:, t0:t0 + n], ps.t[:, 0:n], b.t[:, t0:t0 + n], ALU.mult,
                              [ps.s, b.s], [orw.sl[bi]])
                    kb.dma(sp, aT[0, c * 128:(c + 1) * 128, ta:TA], orw.t[:, ta:TA],
                           [orw.sl[bi] for bi, _, _ in seq_blocks(seqs)], ())
                kb.barrier()

        def p2d_confconv(l, seqs):
            P = 15
            W = TA + 3 * P
            with ExitStack() as st:
                sm = load_small(st, l)
                arow = [kb.tile(st, [128, W], BF16) for _ in range(2)]
                grow = [kb.tile(st, [128, W], BF16) for _ in range(2)]
                orow = [kb.tile(st, [128, TA], BF16, nslots=len(BLOCKS)) for _ in range(2)]
                dgs = [kb.tile(st, [128, 31, 128], BF16) for _ in range(2)]
                for t in arow + grow:
                    kb.memset(pool, t.t[:, :], 0.0, [t.s])
                ta = 0 if "c" in seqs else NCTX
                for c in range(8):
                    a, g, orw, dg = arow[c % 2], grow[c % 2], orow[c % 2], dgs[c % 2]
                    row_load(sp, a, P, zx[ZDA + c * 128:ZDA + (c + 1) * 128, :], seqs, a.s)
                    row_load(sp, g, P, zx[ZDG + c * 128:ZDG + (c + 1) * 128, :], seqs, g.s)
                    kb.tt(dve, a.t[:, :], a.t[:, :], g.t[:, :], ALU.mult, [a.s, g.s], [a.s])
                    make_diags(dve, dg, sm, SP_CD + c * 31, 31)
                    for bi, t0, n in seq_blocks(seqs):
                        ps = next_ps()
                        p0 = poff(t0, P) - P
                        for k in range(31):
                            kb.mm(ps.t[:, 0:n], dg.t[:, k, :], a.t[:, p0 + k:p0 + k + n], k == 0, k == 30,
                                  [dg.s, a.s], [ps.s], k == 30)
                        kb.actf(orw.t[:, t0:t0 + n], ps.t[:, 0:n], AF.Identity, [ps.s, sm.s], [orw.sl[bi]],
                                bias=sm.t[:, SP_CDB + c:SP_CDB + c + 1])
                    kb.dma(sp, aT[3, c * 128:(c + 1) * 128, ta:TA], orw.t[:, ta:TA],
                           [orw.sl[bi] for bi, _, _ in seq_blocks(seqs)], ())
                kb.barrier()

        def p2acd_fused(l, seqs):
            PA, PC, PD = 1, 16, 15
            WA_, WC_, WD_ = TA + 3 * PA, TA + 3 * PC, TA + 3 * PD
            blocks = seq_blocks(seqs)
            bis = [bi for bi, _, _ in blocks]
            ta = 0 if "c" in seqs else NCTX
            with ExitStack() as st:
                sm = load_small(st, l)
                arow = [kb.tile(st, [128, WD_], BF16) for _ in range(2)]
                grow = [kb.tile(st, [128, WD_], BF16) for _ in range(2)]
                orowd = [kb.tile(st, [128, TA], BF16, nslots=len(BLOCKS)) for _ in range(2)]
                dgsd = [kb.tile(st, [128, 31, 128], BF16) for _ in range(2)]
                brow = kb.tile(st, [128, TA], BF16)
                crow = kb.tile(st, [128, WA_], BF16)
                hrow = kb.tile(st, [128, WA_], BF16)
                orowa = kb.tile(st, [128, TA], BF16, nslots=len(BLOCKS))
                dga = kb.tile(st, [128, 3, 128], BF16)
                urow = kb.tile(st, [128, WC_], BF16)
                sa = kb.tile(st, [128, WC_], F32)
                sb = kb.tile(st, [128, WC_], F32)
                inv = kb.tile(st, [128, WC_], F32)
                orowc = kb.tile(st, [128, TA], BF16)
                for t in arow + grow + [crow, hrow, urow, sa, sb]:
                    kb.memset(pool, t.t[:, :], 0.0, [t.s])

                def d_prep(c):
                    a_, g_, dg = arow[c % 2], grow[c % 2], dgsd[c % 2]
                    row_load(sp, a_, PD, zx[ZDA + c * 128:ZDA + (c + 1) * 128, :], seqs, a_.s)
                    row_load(sp, g_, PD, zx[ZDG + c * 128:ZDG + (c + 1) * 128, :], seqs, g_.s)
                    kb.tt(dve, a_.t[:, :], a_.t[:, :], g_.t[:, :], ALU.mult, [a_.s, g_.s], [a_.s])
                    make_diags(dve, dg, sm, SP_CD + c * 31, 31)

                def d_mm(c):
                    a_, orw, dg = arow[c % 2], orowd[c % 2], dgsd[c % 2]
                    for bi, t0, n in blocks:
                        ps = next_ps()
                        p0 = poff(t0, PD) - PD
                        for k in range(31):
                            kb.mm(ps.t[:, 0:n], dg.t[:, k, :], a_.t[:, p0 + k:p0 + k + n], k == 0, k == 30,
                                  [dg.s, a_.s], [ps.s], k == 30)
                        kb.actf(orw.t[:, t0:t0 + n], ps.t[:, 0:n], AF.Identity, [ps.s, sm.s], [orw.sl[bi]],
                                bias=sm.t[:, SP_CDB + c:SP_CDB + c + 1])
                    kb.dma(pool, aT[3, c * 128:(c + 1) * 128, ta:TA], orw.t[:, ta:TA], [orw.sl[b] for b in bis], ())

                def a_prep(c):
                    kb.dma(sp, brow.t[:, ta:TA], zx[ZA + c * 128:ZA + (c + 1) * 128, ta:TA], (), [brow.s])
                    row_load(sp, crow, PA, zx[ZA + 1024 + c * 128:ZA + 1024 + (c + 1) * 128, :], seqs, crow.s)
                    row_load(sp, hrow, PA, zx[ZA + 2048 + c * 128:ZA + 2048 + (c + 1) * 128, :], seqs, hrow.s)
                    kb.tt(pool, crow.t[:, :], crow.t[:, :], hrow.t[:, :], ALU.mult, [crow.s, hrow.s], [crow.s])
                    make_diags(dve, dga, sm, SP_CA + c * 3, 3)

                def a_mm(c):
                    for bi, t0, n in blocks:
                        ps = next_ps()
                        p0 = poff(t0, PA) - PA
                        for k in range(3):
                            kb.mm(ps.t[:, 0:n], dga.t[:, k, :], crow.t[:, p0 + k:p0 + k + n], k == 0, k == 2,
                                  [dga.s, crow.s], [ps.s], k == 2)
                        kb.tt(dve, orowa.t[:, t0:t0 + n], ps.t[:, 0:n], brow.t[:, t0:t0 + n], ALU.mult,
                              [ps.s, brow.s], [orowa.sl[bi]])
                    kb.dma(pool, aT[0, c * 128:(c + 1) * 128, ta:TA], orowa.t[:, ta:TA],
                           [orowa.sl[b] for b in bis], ())

                W = WC_
                P = PC
                tp = poff(ta, P)

                def doubling(src, A, B, nst):
                    kb.tt(dve, A.t[:, 1:W], src.t[:, 0:W - 1], src.t[:, 1:W], ALU.add, [src.s], [A.s])
                    cur, oth = A, B
                    lo, hi = 1, W
                    for sh in (1, 2, 4)[:nst - 1]:
                        nlo, nhi = lo + sh, hi - sh
                        kb.tt(dve, oth.t[:, nlo:nhi], cur.t[:, nlo - sh:nhi - sh], cur.t[:, nlo + sh:nhi + sh],
                              ALU.add, [cur.s], [oth.s])
                        cur, oth = oth, cur
                        lo, hi = nlo, nhi
                    return cur

                def c_all(c):
                    g = c // 2
                    if c % 2 == 0:
                        kb.memset(dve, sa.t[:, :], 0.0, [sa.s])
                        if "c" in seqs:
                            kb.memset(dve, sa.t[:, P:P + NCTX], 1.0, [sa.s])
                        kb.memset(dve, sa.t[:, 2 * P + NCTX:2 * P + TA], 1.0, [sa.s])
                        kb.memset(dve, sb.t[:, :], 0.0, [sb.s])
                        kb.memset(dve, inv.t[:, :], 0.0, [inv.s])
                        cnt = doubling(sa, inv, sb, g + 1)
                        kb.ts(dve, cnt.t[:, :], cnt.t[:, :], 1.0, None, ALU.max, None, [cnt.s], [cnt.s])
                        kb.actf(inv.t[:, :], cnt.t[:, :], AF.Ln, [cnt.s], [inv.s])
                        kb.actf(inv.t[:, :], inv.t[:, :], AF.Exp, [inv.s], [inv.s], scale=-1.0)
                        kb.memset(dve, sa.t[:, :], 0.0, [sa.s])
                        kb.memset(dve, sb.t[:, :], 0.0, [sb.s])
                    row_load(sp, urow, P, zx[ZP + c * 128:ZP + (c + 1) * 128, :], seqs, urow.s)
                    s_ = doubling(urow, sa, sb, g + 1)
                    kb.tt(dve, s_.t[:, tp:2 * P + TA], s_.t[:, tp:2 * P + TA], inv.t[:, tp:2 * P + TA], ALU.mult,
                          [s_.s, inv.s], [s_.s])
                    if "c" in seqs:
                        kb.tt(dve, orowc.t[:, 0:NCTX], s_.t[:, P:P + NCTX], urow.t[:, P:P + NCTX], ALU.subtract,
                              [s_.s, urow.s], [orowc.s])
                    kb.tt(dve, orowc.t[:, NCTX:TA], s_.t[:, 2 * P + NCTX:2 * P + TA],
                          urow.t[:, 2 * P + NCTX:2 * P + TA], ALU.subtract, [s_.s, urow.s], [orowc.s])
                    kb.dma(pool, aT[2, c * 128:(c + 1) * 128, ta:TA], orowc.t[:, ta:TA], [orowc.s], ())

                d_prep(0)
                for c in range(8):
                    c_all(c)
                    if c + 1 < 8:
                        d_prep(c + 1)
                    a_prep(c)
                    d_mm(c)
                    a_mm(c)
                kb.barrier()

        def p2c_pool(l, seqs):
            P = 16
            W = TA + 3 * P
            with ExitStack() as st:
                urow = [kb.tile(st, [128, W], BF16) for _ in range(2)]
                sa = [kb.tile(st, [128, W], F32) for _ in range(2)]
                sb = [kb.tile(st, [128, W], F32) for _ in range(2)]
                inv = kb.tile(st, [128, W], F32)
                ca = kb.tile(st, [128, W], F32)
                cb = kb.tile(st, [128, W], F32)
                orow = [kb.tile(st, [128, TA], BF16) for _ in range(2)]
                for t in urow + sa + sb + [ca, cb]:
                    kb.memset(pool, t.t[:, :], 0.0, [t.s])
                ta = 0 if "c" in seqs else NCTX
                tp = poff(ta, P)

                def doubling(eng, src, A, B, nst):
                    kb.tt(eng, A.t[:, 1:W], src.t[:, 0:W - 1], src.t[:, 1:W], ALU.add, [src.s], [A.s])
                    cur, oth = A, B
                    lo, hi = 1, W
                    for sh in (1, 2, 4)[:nst - 1]:
                        nlo, nhi = lo + sh, hi - sh
                        kb.tt(eng, oth.t[:, nlo:nhi], cur.t[:, nlo - sh:nhi - sh], cur.t[:, nlo + sh:nhi + sh],
                              ALU.add, [cur.s], [oth.s])
                        cur, oth = oth, cur
                        lo, hi = nlo, nhi
                    return cur

                for g in range(4):
                    kb.memset(dve, ca.t[:, :], 0.0, [ca.s])
                    if "c" in seqs:
                        kb.memset(dve, ca.t[:, P:P + NCTX], 1.0, [ca.s])
                    kb.memset(dve, ca.t[:, 2 * P + NCTX:2 * P + TA], 1.0, [ca.s])
                    kb.memset(dve, cb.t[:, :], 0.0, [cb.s])
                    kb.memset(dve, inv.t[:, :], 0.0, [inv.s])
                    cnt = doubling(dve, ca, inv, cb, g + 1)
                    kb.ts(dve, cnt.t[:, :], cnt.t[:, :], 1.0, None, ALU.max, None, [cnt.s], [cnt.s])
                    if cnt is not inv:
                        kb.op(dve, lambda e: e.reciprocal(out=inv.t[:, :], in_=cnt.t[:, :]), [cnt.s], [inv.s])
                    else:
                        kb.op(dve, lambda e: e.reciprocal(out=inv.t[:, :], in_=inv.t[:, :]), [inv.s], [inv.s])
                    for cc_ in range(2):
                        c = g * 2 + cc_
                        eng = dve if cc_ == 0 else pool
                        u, A, B, orw = urow[cc_], sa[cc_], sb[cc_], orow[cc_]
                        row_load(sp, u, P, zx[ZP + c * 128:ZP + (c + 1) * 128, :], seqs, u.s)
                        s = doubling(eng, u, A, B, g + 1)
                        kb.tt(eng, s.t[:, tp:2 * P + TA], s.t[:, tp:2 * P + TA], inv.t[:, tp:2 * P + TA], ALU.mult,
                              [s.s, inv.s], [s.s])
                        if "c" in seqs:
                            kb.tt(eng, orw.t[:, 0:NCTX], s.t[:, P:P + NCTX], u.t[:, P:P + NCTX], ALU.subtract,
                                  [s.s, u.s], [orw.s])
                        kb.tt(eng, orw.t[:, NCTX:TA], s.t[:, 2 * P + NCTX:2 * P + TA], u.t[:, 2 * P + NCTX:2 * P + TA],
                              ALU.subtract, [s.s, u.s], [orw.s])
                        kb.dma(sp, aT[2, c * 128:(c + 1) * 128, ta:TA], orw.t[:, ta:TA], [orw.s], ())
                kb.barrier()

        def rstd_from_ps(eng, dst, ps, n, inv_dim, eps, dslot, pslot):
            eb = epsb if eps == EPS else lnepsb
            kb.actf(dst.t[:, 0:n], ps.t[:, 0:n], AF.Ln, [pslot], [dslot], bias=eb.t[:, 0:1], scale=inv_dim)
            kb.actf(dst.t[:, 0:n], dst.t[:, 0:n], AF.Exp, [dslot], [dslot], scale=-0.5)

        def p2b1_attnprep(l, seqs):
            with ExitStack() as st:
                sm = load_small(st, l)
                wq = kb.tile(st, [128, 6, 1536], BF16)
                wkv = kb.tile(st, [128, 4, 2048], BF16)
                kb.dma(pool, wq.t[:, :, :], w_qup[l].rearrange("(kc p) n -> p kc n", p=128), (), [wq.s])
                kb.dma(pool, wkv.t[:, :, :], w_kvup[l].rearrange("(kc p) n -> p kc n", p=128), (), [wkv.s])
                zq = [kb.tile(st, [128, 6, 512], BF16) for _ in range(2)]
                zkv = [kb.tile(st, [128, 4, 512], BF16) for _ in range(2)]
                zkr = [kb.tile(st, [128, 512], BF16) for _ in range(2)]
                rc = [kb.tile(st, [128, 512], F32) for _ in range(2)]
                rsn = [kb.tile(st, [128, 512], F32) for _ in range(2)]

                class TS:
                    pass

                def mk(nsq, nzn):
                    T = TS()
                    T.sq = kb.tile(st, [128, nsq, 512], BF16, nslots=nsq)
                    T.raw = kb.tile(st, [128, nsq, 512], BF16, nslots=nsq)
                    T.zn = kb.tile(st, [128, nzn, 512], BF16)
                    T.rstd = kb.tile(st, [128, 512], F32)
                    T.rh = [kb.tile(st, [128, 512], F32) for _ in range(2)]
                    T.t1 = kb.tile(st, [128, 512], F32)
                    T.t2 = kb.tile(st, [128, 512], F32)
                    T.rr = kb.tile(st, [128, 512], BF16)
                    T.qo = [kb.tile(st, [128, 512], BF16) for _ in range(4)]
                    T.qoi = 0
                    return T

                TQ = mk(12, 6)
                TK = mk(9, 4)
                rrot = kb.tile(st, [128, 512], F32)
                vo = [kb.tile(st, [128, 1024], BF16) for _ in range(2)]

                def next_qo(T):
                    q = T.qo[T.qoi % 4]
                    T.qoi += 1
                    return q

                def loads(bi):
                    t0, n = BLOCKS[bi]
                    isx = bi > 0
                    do_q = isx or ("c" in seqs)
                    b2 = bi % 2
                    Zq, Zkv, Zkr, Rc, Rs = zq[b2], zkv[b2], zkr[b2], rc[b2], rsn[b2]
                    if do_q:
                        kb.dma(sp, Zq.t[:, :, 0:n], zx[ZQ:ZQ + 768, t0:t0 + n].rearrange("(c p) t -> p c t", p=128),
                               (), [Zq.s])
                    kb.dma(sp, Zkv.t[:, :, 0:n], zx[ZKV:ZKV + 512, t0:t0 + n].rearrange("(c p) t -> p c t", p=128),
                           (), [Zkv.s])
                    kb.dma(sp, Zkr.t[0:64, 0:n], zx[ZKR:ZKR + 64, t0:t0 + n], (), [Zkr.s])
                    kb.dma(sp, Zkr.t[64:128, 0:n], zx[ZKR:ZKR + 64, t0:t0 + n], (), [Zkr.s])
                    if isx:
                        kb.dma(sp, Rc.t[:, 0:n], rope_c[:, t0 - NCTX:t0 - NCTX + n], (), [Rc.s])
                        kb.dma(sp, Rs.t[:, 0:n], rope_s[:, t0 - NCTX:t0 - NCTX + n], (), [Rs.s])

                def lowrank_norm(T, Z, nch, inv_dim, gcol, n):
                    for c in range(nch):
                        kb.actf(T.sq.t[:, c, 0:n], Z.t[:, c, 0:n], AF.Square, [Z.s], [T.sq.sl[c]])
                    ps = next_ps()
                    for c in range(nch):
                        kb.mm(ps.t[:, 0:n], ones.t[:, :], T.sq.t[:, c, 0:n], c == 0, c == nch - 1,
                              [ones.s, T.sq.sl[c]], [ps.s], c == nch - 1)
                    rstd_from_ps(dve, T.rstd, ps, n, inv_dim, EPS, T.rstd.s, ps.s)
                    for c in range(nch):
                        kb.stt(dve, T.zn.t[:, c, 0:n], Z.t[:, c, 0:n], sm.t[:, gcol + c:gcol + c + 1],
                               T.rstd.t[:, 0:n], ALU.mult, ALU.mult, [Z.s, sm.s, T.rstd.s], [T.zn.s])

                def rope_rot(T, Rc, Rs, srcT, srcslot, dstT, dstslot, n):
                    ps = next_ps()
                    kb.mm(ps.t[:, 0:n], pm.t[:, :], srcT[:, 0:n], True, True, [pm.s, srcslot], [ps.s], True)
                    a, b_ = T.t1, T.t2
                    kb.tt(dve, a.t[:, 0:n], srcT[:, 0:n], Rc.t[:, 0:n], ALU.mult, [srcslot, Rc.s], [a.s])
                    kb.tt(dve, b_.t[:, 0:n], ps.t[:, 0:n], Rs.t[:, 0:n], ALU.mult, [ps.s, Rs.s], [b_.s])
                    kb.tt(pool, dstT[:, 0:n], a.t[:, 0:n], b_.t[:, 0:n], ALU.add, [a.s, b_.s], [dstslot])

                def q_path(bi):
                    t0, n = BLOCKS[bi]
                    isx = bi > 0
                    b2 = bi % 2
                    T = TQ
                    Zq, Rc, Rs = zq[b2], rc[b2], rsn[b2]
                    lowrank_norm(T, Zq, 6, 1.0 / 768, SP_QN, n)
                    yield
                    for oc in range(12):
                        ps = next_ps()
                        for kc in range(6):
                            kb.mm(ps.t[:, 0:n], wq.t[:, kc, oc * 128:(oc + 1) * 128], T.zn.t[:, kc, 0:n],
                                  kc == 0, kc == 5, [wq.s, T.zn.s], [ps.s], kc == 5)
                        kb.actf(T.sq.t[:, oc, 0:n], ps.t[:, 0:n], AF.Square, [ps.s], [T.sq.sl[oc]])
                        kb.copy(act if oc % 2 == 0 else dve, T.raw.t[:, oc, 0:n], ps.t[:, 0:n], [ps.s], [T.raw.sl[oc]])
                        yield
                    for j in range(4):
                        qr = next_qo(T)
                        for hh_ in range(2):
                            h = 2 * j + hh_
                            base = hh_ * 64
                            ps = next_ps()
                            kb.mm(ps.t[:, 0:n], ones.t[:, :], T.sq.t[:, h, 0:n], True, False,
                                  [ones.s, T.sq.sl[h]], [ps.s], False)
                            kb.mm(ps.t[:, 0:n], ones.t[base:base + 64, :], T.sq.t[base:base + 64, 8 + j, 0:n],
                                  False, True, [ones.s, T.sq.sl[8 + j]], [ps.s], True)
                            R = T.rh[hh_]
                            rstd_from_ps(dve, R, ps, n, 1.0 / 192, EPS, R.s, ps.s)
                            qn = next_qo(T)
                            kb.stt(dve, qn.t[:, 0:n], T.raw.t[:, h, 0:n], sm.t[:, SP_QHN:SP_QHN + 1], R.t[:, 0:n],
                                   ALU.mult, ALU.mult, [T.raw.sl[h], sm.s, R.s], [qn.s])
                            kb.dma(pool, qT[h * 128:(h + 1) * 128, t0:t0 + n], qn.t[:, 0:n], [qn.s], ())
                            kb.stt(dve, T.rr.t[base:base + 64, 0:n], T.raw.t[base:base + 64, 8 + j, 0:n],
                                   sm.t[base:base + 64, SP_QHR:SP_QHR + 1], R.t[base:base + 64, 0:n],
                                   ALU.mult, ALU.mult, [T.raw.sl[8 + j], sm.s, R.s], [T.rr.s])
                            yield
                        if isx:
                            rope_rot(T, Rc, Rs, T.rr.t, T.rr.s, qr.t, qr.s, n)
                        else:
                            kb.copy(pool, qr.t[:, 0:n], T.rr.t[:, 0:n], [T.rr.s], [qr.s])
                        kb.dma(pool, qT[1024 + j * 128:1024 + (j + 1) * 128, t0:t0 + n], qr.t[:, 0:n], [qr.s], ())
                        yield

                def kv_path(bi):
                    t0, n = BLOCKS[bi]
                    isx = bi > 0
                    b2 = bi % 2
                    T = TK
                    Zkv, Zkr, Rc, Rs = zkv[b2], zkr[b2], rc[b2], rsn[b2]
                    lowrank_norm(T, Zkv, 4, 1.0 / 512, SP_KVN, n)
                    yield
                    for tt in range(n // 128):
                        v = vo[tt % 2]
                        for half in range(2):
                            ps = next_ps()
                            for kc in range(4):
                                kb.mm(ps.t[:, :], T.zn.t[:, kc, tt * 128:(tt + 1) * 128],
                                      wkv.t[:, kc, 1024 + half * 512:1024 + (half + 1) * 512],
                                      kc == 0, kc == 3, [T.zn.s, wkv.s], [ps.s], kc == 3)
                            kb.copy(act if half == 0 else dve, v.t[:, half * 512:(half + 1) * 512], ps.t[:, :],
                                    [ps.s], [v.s])
                        kb.dma(pool, vv[t0 + tt * 128:t0 + (tt + 1) * 128, :], v.t[:, :], [v.s], ())
                        yield
                    kb.actf(T.sq.t[:, 8, 0:n], Zkr.t[:, 0:n], AF.Square, [Zkr.s], [T.sq.sl[8]])
                    kb.ts(dve, T.rr.t[:, 0:n], Zkr.t[:, 0:n], sm.t[:, SP_KHR:SP_KHR + 1], None, ALU.mult, None,
                          [Zkr.s, sm.s], [T.rr.s])
                    if isx:
                        rope_rot(T, Rc, Rs, T.rr.t, T.rr.s, rrot.t, rrot.s, n)
                    else:
                        kb.copy(pool, rrot.t[:, 0:n], T.rr.t[:, 0:n], [T.rr.s], [rrot.s])
                    yield
                    for oc in range(8):
                        ps = next_ps()
                        for kc in range(4):
                            kb.mm(ps.t[:, 0:n], wkv.t[:, kc, oc * 128:(oc + 1) * 128], T.zn.t[:, kc, 0:n],
                                  kc == 0, kc == 3, [wkv.s, T.zn.s], [ps.s], kc == 3)
                        kb.actf(T.sq.t[:, oc, 0:n], ps.t[:, 0:n], AF.Square, [ps.s], [T.sq.sl[oc]])
                        kb.copy(act if oc % 2 == 0 else dve, T.raw.t[:, oc, 0:n], ps.t[:, 0:n], [ps.s], [T.raw.sl[oc]])
                        yield
                    for j in range(4):
                        kr = next_qo(T)
                        for hh_ in range(2):
                            h = 2 * j + hh_
                            base = hh_ * 64
                            ps = next_ps()
                            kb.mm(ps.t[:, 0:n], ones.t[:, :], T.sq.t[:, h, 0:n], True, False,
                                  [ones.s, T.sq.sl[h]], [ps.s], False)
                            kb.mm(ps.t[:, 0:n], ones.t[0:64, :], T.sq.t[0:64, 8, 0:n], False, True,
                                  [ones.s, T.sq.sl[8]], [ps.s], True)
                            R = T.rh[hh_]
                            rstd_from_ps(dve, R, ps, n, 1.0 / 192, EPS, R.s, ps.s)
                            kn = next_qo(T)
                            kb.stt(dve, kn.t[:, 0:n], T.raw.t[:, h, 0:n], sm.t[:, SP_KHN:SP_KHN + 1], R.t[:, 0:n],
                                   ALU.mult, ALU.mult, [T.raw.sl[h], sm.s, R.s], [kn.s])
                            kb.dma(pool, kT[h * 128:(h + 1) * 128, t0:t0 + n], kn.t[:, 0:n], [kn.s], ())
                            kb.tt(pool, kr.t[base:base + 64, 0:n], rrot.t[base:base + 64, 0:n],
                                  R.t[base:base + 64, 0:n], ALU.mult, [rrot.s, R.s], [kr.s])
                            yield
                        kb.dma(pool, kT[1024 + j * 128:1024 + (j + 1) * 128, t0:t0 + n], kr.t[:, 0:n], [kr.s], ())
                        yield

                loads(0)
                for bi in range(len(BLOCKS)):
                    if bi + 1 < len(BLOCKS):
                        loads(bi + 1)
                    isx = bi > 0
                    gens = [kv_path(bi)]
                    if isx or ("c" in seqs):
                        gens.insert(0, q_path(bi))
                    while gens:
                        for g_ in list(gens):
                            try:
                                next(g_)
                            except StopIteration:
                                gens.remove(g_)
                kb.barrier()

        def p2b2_attn(l, seqs):
            scale = 192.0 ** -0.5
            with ExitStack() as st:
                kn = [kb.tile(st, [128, TA], BF16) for _ in range(2)]
                kr = [kb.tile(st, [128, TA], BF16) for _ in range(2)]
                vh = [kb.tile(st, [128, NTT, 128], BF16) for _ in range(2)]
                qn = [kb.tile(st, [128, 512], BF16) for _ in range(2)]
                qr = [kb.tile(st, [128, 512], BF16) for _ in range(2)]
                pt = [kb.tile(st, [128, 512], BF16) for _ in range(4)]
                rcp = [kb.tile(st, [128, 512], F32) for _ in range(2)]
                ob = [kb.tile(st, [128, 512], BF16) for _ in range(2)]
                ps_s = psum[0:4]
                ps_o = psum[4:6]
                ps_l = psum[6:8]
                it = 0
                qi = 0
                for t_ in kr + qr:
                    kb.memset(pool, t_.t[:, :], 0.0, [t_.s])
                for h in range(8):
                    base = (h % 2) * 64
                    j = h // 2
                    K, KR, V = kn[h % 2], kr[h % 2], vh[h % 2]
                    ob_ = 64 - base
                    for t_ in qr:
                        kb.memset(pool, t_.t[ob_:ob_ + 64, :], 0.0, [t_.s])
                    kb.dma(sp, K.t[:, :], kT[h * 128:(h + 1) * 128, :], (), [K.s])
                    kb.dma(sp, KR.t[base:base + 64, :], kT[1024 + j * 128 + base:1024 + j * 128 + base + 64, :],
                           (), [KR.s])
                    vsrc = vv[:, h * 128:(h + 1) * 128].rearrange("(t p) c -> p t c", p=128)
                    for part in range(2):
                        kb.dma(sp, V.t[:, part * 17:(part + 1) * 17, :], vsrc[:, part * 17:(part + 1) * 17, :],
                               (), [V.s])
                    for bi, t0, n in seq_blocks(seqs):
                        Q, QR = qn[qi % 2], qr[qi % 2]
                        po, pl = ps_o[qi % 2], ps_l[qi % 2]
                        rc_, o_ = rcp[qi % 2], ob[qi % 2]
                        qi += 1
                        kb.dma(sp, Q.t[:, 0:n], qT[h * 128:(h + 1) * 128, t0:t0 + n], (), [Q.s])
                        kb.dma(sp, QR.t[base:base + 64, 0:n],
                               qT[1024 + j * 128 + base:1024 + j * 128 + base + 64, t0:t0 + n], (), [QR.s])
                        nkt = 2 if bi == 0 else NTT
                        LOOK = 2
                        stage = {}
                        for i in range(nkt + LOOK):
                            if i < nkt:
                                kt = i
                                ps = ps_s[it % 4]
                                p_ = pt[it % 4]
                                it += 1
                                stage[kt] = p_
                                kb.mm(ps.t[:, 0:n], K.t[:, kt * 128:(kt + 1) * 128], Q.t[:, 0:n], True, False,
                                      [K.s, Q.s], [ps.s], False)
                                kb.mm(ps.t[:, 0:n], KR.t[:, kt * 128:(kt + 1) * 128],
                                      QR.t[:, 0:n], False, True, [KR.s, QR.s], [ps.s], True)
                                kb.actf(p_.t[:, 0:n], ps.t[:, 0:n], AF.Exp, [ps.s], [p_.s], scale=scale)
                            if i >= LOOK:
                                kt = i - LOOK
                                p_ = stage.pop(kt)
                                kb.mm(po.t[:, 0:n], V.t[:, kt, :], p_.t[:, 0:n], kt == 0, kt == nkt - 1,
                                      [V.s, p_.s], [po.s], False)
                                kb.mm(pl.t[:, 0:n], ones.t[:, :], p_.t[:, 0:n], kt == 0, kt == nkt - 1,
                                      [ones.s, p_.s], [pl.s], kt == nkt - 1)
                        kb.op(dve, lambda e, rc_=rc_, pl=pl, n=n: e.reciprocal(out=rc_.t[:, 0:n], in_=pl.t[:, 0:n]),
                              [pl.s], [rc_.s])
                        kb.tt(dve, o_.t[:, 0:n], po.t[:, 0:n], rc_.t[:, 0:n], ALU.mult, [po.s, rc_.s], [o_.s])
                        kb.dma(sp, aT[1, h * 128:(h + 1) * 128, t0:t0 + n], o_.t[:, 0:n], [o_.s], ())
                kb.barrier()

        def p2m_merge(l, seqs, xsrc, xdst):
            with ExitStack() as st:
                sm = load_small(st, l)
                NW = 5
                wr = [kb.tile(st, [128, 8, 1024], BF16) for _ in range(NW)]
                wi = [0]

                def next_w():
                    w = wr[wi[0] % NW]
                    wi[0] += 1
                    return w

                macc = kb.tile(st, [128, 16, 512], F32, nslots=16)
                mbf = kb.tile(st, [128, 16, 512], BF16, nslots=16)
                ain = [kb.tile(st, [128, 8, 512], BF16) for _ in range(2)]
                ysq = kb.tile(st, [128, 8, 512], BF16, nslots=8)
                mean = kb.tile(st, [128, 512], F32)
                rstd = kb.tile(st, [128, 512], F32)
                msq = kb.tile(st, [128, 512], F32)
                ctr = [kb.tile(st, [128, 512], F32) for _ in range(2)]
                gt_ = [kb.tile(st, [128, 4, 512], BF16) for _ in range(2)]
                tmp = [kb.tile(st, [128, 512], F32) for _ in range(3)]
                xio = [kb.tile(st, [128, 1024], F32) for _ in range(3)]
                g1b = kb.tile(st, [128, D], F32)
                gi = [0]
                ti = [0]
                xi = [0]
                cur = [None]
                ains = {}
                wv3 = [w_aout[l, i].rearrange("(kc p) n -> p kc n", p=128) for i in range(3)]
                wvp = w_pool[l].rearrange("(gk p) n -> p gk n", p=128)
                wvo = w_out[l].rearrange("(kc p) n -> p kc n", p=128)

                def ld_branch(br, half):
                    def f():
                        w = next_w()
                        if br == 2:
                            kb.dma(pool, w.t[:, :, 0:512], wvp, (), [w.s])
                        else:
                            widx = {0: 0, 1: 1, 3: 2}[br]
                            kb.dma(pool, w.t[:, :, :], wv3[widx][:, :, half * 1024:(half + 1) * 1024], (), [w.s])
                        return (w,)
                    return f

                def ld_out(ch):
                    def f():
                        wlo_ = next_w()
                        whi_ = next_w()
                        kb.dma(pool, wlo_.t[:, :, :], wvo[:, 0:8, ch * 1024:(ch + 1) * 1024], (), [wlo_.s])
                        kb.dma(pool, whi_.t[:, :, :], wvo[:, 8:16, ch * 1024:(ch + 1) * 1024], (), [whi_.s])
                        return (wlo_, whi_)
                    return f

                def prep_load(bi, t0, n, br):
                    A = ain[br % 2]
                    ains[(bi, br)] = A
                    kb.dma(sp, A.t[:, :, 0:n], aT[br, :, t0:t0 + n].rearrange("(c p) t -> p c t", p=128),
                           (), [A.s])

                def prep_compute(bi, t0, n, br):
                    A = ains[(bi, br)]
                    if br == 3:
                        for c in range(8):
                            kb.actf(ysq.t[:, c, 0:n], A.t[:, c, 0:n], AF.Square, [A.s], [ysq.sl[c]])
                        p1 = next_ps()
                        p2 = next_ps()
                        for c in range(8):
                            kb.mm(p1.t[:, 0:n], ones.t[:, :], A.t[:, c, 0:n], c == 0, c == 7,
                                  [ones.s, A.s], [p1.s], c == 7)
                        for c in range(8):
                            kb.mm(p2.t[:, 0:n], ones.t[:, :], ysq.t[:, c, 0:n], c == 0, c == 7,
                                  [ones.s, ysq.sl[c]], [p2.s], c == 7)
                        kb.ts(dve, mean.t[:, 0:n], p1.t[:, 0:n], 1.0 / 1024, None, ALU.mult, None,
                              [p1.s], [mean.s])
                        kb.tt(dve, msq.t[:, 0:n], mean.t[:, 0:n], mean.t[:, 0:n], ALU.mult, [mean.s], [msq.s])
                        kb.stt(dve, rstd.t[:, 0:n], p2.t[:, 0:n], 1.0 / 1024, msq.t[:, 0:n], ALU.mult,
                               ALU.subtract, [p2.s, msq.s], [rstd.s])
                        kb.actf(rstd.t[:, 0:n], rstd.t[:, 0:n], AF.Sqrt, [rstd.s], [rstd.s],
                                bias=lnepsb.t[:, 0:1], scale=1.0)
                        kb.op(dve, lambda e: e.reciprocal(out=rstd.t[:, 0:n], in_=rstd.t[:, 0:n]),
                              [rstd.s], [rstd.s])
                        for c in range(8):
                            ct = ctr[c % 2]
                            kb.tt(pool, ct.t[:, 0:n], A.t[:, c, 0:n], mean.t[:, 0:n], ALU.subtract,
                                  [A.s, mean.s], [ct.s])
                            kb.tt(dve, ct.t[:, 0:n], ct.t[:, 0:n], rstd.t[:, 0:n], ALU.mult,
                                  [ct.s, rstd.s], [ct.s])
                            kb.actf(A.t[:, c, 0:n], ct.t[:, 0:n], AF.Silu, [ct.s, sm.s], [A.s],
                                    bias=sm.t[:, SP_LNB + c:SP_LNB + c + 1],
                                    scale=sm.t[:, SP_LNG + c:SP_LNG + c + 1])

                def cp_branch(bi, t0, n, br, half):
                    def f(wt):
                        (w,) = wt
                        j = branches.index((bi, t0, n, br))
                        if half == 0:
                            if j == 0:
                                prep_load(bi, t0, n, br)
                                prep_compute(bi, t0, n, br)
                            if j + 1 < len(branches):
                                prep_load(*branches[j + 1])
                        else:
                            if j + 1 < len(branches):
                                prep_compute(*branches[j + 1])
                        A = ains[(bi, br)]
                        for o8 in range(8):
                            oc = half * 8 + o8
                            ps = next_ps()
                            if br == 2:
                                g = oc // 4
                                for k2 in range(2):
                                    kb.mm(ps.t[:, 0:n], w.t[:, g * 2 + k2, (oc % 4) * 128:(oc % 4 + 1) * 128],
                                          A.t[:, g * 2 + k2, 0:n], k2 == 0, k2 == 1, [w.s, A.s], [ps.s], k2 == 1)
                            else:
                                for kc in range(8):
                                    kb.mm(ps.t[:, 0:n], w.t[:, kc, o8 * 128:(o8 + 1) * 128], A.t[:, kc, 0:n],
                                          kc == 0, kc == 7, [w.s, A.s], [ps.s], kc == 7)
                            if oc % 4 == 0:
                                gtile = gt_[gi[0] % 2]
                                gi[0] += 1
                                r0 = ZG + br * D + oc * 128
                                kb.dma(sp, gtile.t[:, :, 0:n],
                                       zx[r0:r0 + 512, t0:t0 + n].rearrange("(c p) t -> p c t", p=128), (),
                                       [gtile.s])
                                cp_branch.gtile = gtile
                            gtile = cp_branch.gtile
                            gta = gtile.t[:, oc % 4, 0:n]
                            if br == 0:
                                kb.tt(dve, macc.t[:, oc, 0:n], ps.t[:, 0:n], gta, ALU.mult,
                                      [ps.s, gtile.s], [macc.sl[oc]])
                            else:
                                tm = tmp[ti[0] % 3]
                                ti[0] += 1
                                if br == 2:
                                    kb.stt(dve, tm.t[:, 0:n], ps.t[:, 0:n], sm.t[:, SP_PS + oc:SP_PS + oc + 1],
                                           gta, ALU.mult, ALU.mult, [ps.s, sm.s, gtile.s], [tm.s])
                                else:
                                    kb.tt(dve, tm.t[:, 0:n], ps.t[:, 0:n], gta, ALU.mult,
                                          [ps.s, gtile.s], [tm.s])
                                if br < 3:
                                    kb.tt(pool, macc.t[:, oc, 0:n], macc.t[:, oc, 0:n], tm.t[:, 0:n], ALU.add,
                                          [macc.sl[oc], tm.s], [macc.sl[oc]])
                                else:
                                    kb.tt(pool, mbf.t[:, oc, 0:n], macc.t[:, oc, 0:n], tm.t[:, 0:n], ALU.add,
                                          [macc.sl[oc], tm.s], [mbf.sl[oc]])
                    return f

                def cp_out(bi, t0, n, ch):
                    def f(wt):
                        wl_, wh_ = wt
                        which = 1 if bi == 0 else 0
                        if which != cur[0]:
                            cur[0] = which
                            kb.dma(sp, g1b.t[:, :], modd[which:which + 1, 2 * D:3 * D].partition_broadcast(128),
                                   (), [g1b.s])
                        for tt in range(n // 128):
                            X = xio[xi[0] % 3]
                            xi[0] += 1
                            kb.dma(sp, X.t[:, :], xsrc[t0 + tt * 128:t0 + (tt + 1) * 128, ch * 1024:(ch + 1) * 1024],
                                   (), [X.s])
                            for cg in range(2):
                                ps = next_ps()
                                for kc in range(16):
                                    w_ = wl_ if kc < 8 else wh_
                                    kb.mm(ps.t[:, :], mbf.t[:, kc, tt * 128:(tt + 1) * 128],
                                          w_.t[:, kc % 8, cg * 512:(cg + 1) * 512], kc == 0, kc == 15,
                                          [mbf.sl[kc], w_.s], [ps.s], kc == 15)
                                c0 = ch * 1024 + cg * 512
                                tm = tmp[ti[0] % 3]
                                ti[0] += 1
                                kb.tt(dve, tm.t[:, :], ps.t[:, :], g1b.t[:, c0:c0 + 512], ALU.mult,
                                      [ps.s, g1b.s], [tm.s])
                                kb.tt(pool, X.t[:, cg * 512:(cg + 1) * 512], X.t[:, cg * 512:(cg + 1) * 512],
                                      tm.t[:, :], ALU.add, [X.s, tm.s], [X.s])
                            kb.dma(act, xdst[t0 + tt * 128:t0 + (tt + 1) * 128, ch * 1024:(ch + 1) * 1024],
                                   X.t[:, :], [X.s], ())
                    return f

                items = []
                branches = [(bi, t0, n, br) for bi, t0, n in seq_blocks(seqs) for br in range(4)]
                for bi, t0, n in seq_blocks(seqs):
                    for br in range(4):
                        if br == 2:
                            items.append((ld_branch(2, 0), cp_branch(bi, t0, n, 2, 0)))
                            items.append((None, cp_branch(bi, t0, n, 2, 1)))
                        else:
                            for half in range(2):
                                items.append((ld_branch(br, half), cp_branch(bi, t0, n, br, half)))
                    for ch in range(2):
                        items.append((ld_out(ch), cp_out(bi, t0, n, ch)))
                PF = 2
                loaded = {}
                last = None
                nxt = 0
                for k, (ldf, cpf) in enumerate(items):
                    while nxt < len(items) and nxt <= k + PF:
                        lf = items[nxt][0]
                        if lf is not None:
                            loaded[nxt] = lf()
                        nxt += 1
                    if ldf is not None:
                        last = loaded.pop(k)
                    cpf(last)
                kb.barrier()

        def p3c_ffconv(l, seqs):
            P = 1
            W = TA + 3 * P
            with ExitStack() as st:
                sm = load_small(st, l)
                grow = [kb.tile(st, [128, W], BF16) for _ in range(2)]
                vrow = [kb.tile(st, [128, TA], BF16) for _ in range(2)]
                orow = [kb.tile(st, [128, TA], BF16, nslots=len(BLOCKS)) for _ in range(2)]
                dgs = [kb.tile(st, [128, 3, 128], BF16) for _ in range(2)]
                tmp = [kb.tile(st, [128, 512], F32) for _ in range(3)]
                for t in grow:
                    kb.memset(pool, t.t[:, :], 0.0, [t.s])
                ta = 0 if "c" in seqs else NCTX
                ti = 0
                for c in range(NFC):
                    g, v, orw, dg = grow[c % 2], vrow[c % 2], orow[c % 2], dgs[c % 2]
                    row_load(sp, g, P, ffT[c * 128:(c + 1) * 128, :], seqs, g.s)
                    kb.dma(sp, v.t[:, ta:TA], ffT[DFF + c * 128:DFF + (c + 1) * 128, ta:TA], (), [v.s])
                    make_diags(dve, dg, sm, SP_CF + c * 3, 3)
                    for bi, t0, n in seq_blocks(seqs):
                        ps = next_ps()
                        p0 = poff(t0, P) - P
                        for k in range(3):
                            kb.mm(ps.t[:, 0:n], dg.t[:, k, :], g.t[:, p0 + k:p0 + k + n], k == 0, k == 2,
                                  [dg.s, g.s], [ps.s], k == 2)
                        tm = tmp[ti % 3]
                        ti += 1
                        kb.actf(tm.t[:, 0:n], ps.t[:, 0:n], AF.Silu, [ps.s, sm.s], [tm.s],
                                bias=sm.t[:, SP_CFB + c:SP_CFB + c + 1])
                        kb.tt(dve, orw.t[:, t0:t0 + n], tm.t[:, 0:n], v.t[:, t0:t0 + n], ALU.mult,
                              [tm.s, v.s], [orw.sl[bi]])
                    kb.dma(pool, fT[c * 128:(c + 1) * 128, ta:TA], orw.t[:, ta:TA],
                           [orw.sl[bi] for bi, _, _ in seq_blocks(seqs)], ())
                kb.barrier()

        def p3d_down(l, seqs, xsrc, xdst, dst_is_out):
            with ExitStack() as st:
                wd = kb.tile(st, [128, NFC, 1024], BF16)
                fin = [kb.tile(st, [128, NFC, 128], BF16) for _ in range(3)]
                xio = [kb.tile(st, [128, 1024], F32) for _ in range(3)]
                tmp = [kb.tile(st, [128, 512], F32) for _ in range(3)]
                g2b = kb.tile(st, [128, D], F32)
                wv = w_down[l].rearrange("(kc p) n -> p kc n", p=128)
                ti = 0
                i = 0
                for ch in range(2):
                    for part in range(4):
                        kb.dma(pool, wd.t[:, part * 11:(part + 1) * 11, :],
                               wv[:, part * 11:(part + 1) * 11, ch * 1024:(ch + 1) * 1024], (), [wd.s])
                    cur = None
                    for tt in range(NTT):
                        which = 1 if tt < 2 else 0
                        if which == 1 and "c" not in seqs:
                            continue
                        if which != cur:
                            cur = which
                            kb.dma(sp, g2b.t[:, :], modd[which:which + 1, 5 * D:6 * D].partition_broadcast(128),
                                   (), [g2b.s])
                        t0 = tt * 128
                        F = fin[i % 3]
                        X = xio[i % 3]
                        i += 1
                        fsrc = fT[:, t0:t0 + 128].rearrange("(c p) t -> p c t", p=128)
                        for part in range(4):
                            kb.dma(sp, F.t[:, part * 11:(part + 1) * 11, :], fsrc[:, part * 11:(part + 1) * 11, :],
                                   (), [F.s])
                        kb.dma(sp, X.t[:, :], xsrc[t0:t0 + 128, ch * 1024:(ch + 1) * 1024], (), [X.s])
                        for cg in range(2):
                            ps = next_ps()
                            for kc in range(NFC):
                                kb.mm(ps.t[:, :], F.t[:, kc, :], wd.t[:, kc, cg * 512:(cg + 1) * 512],
                                      kc == 0, kc == NFC - 1, [F.s, wd.s], [ps.s], kc == NFC - 1)
                            tm = tmp[ti % 3]
                            ti += 1
                            c0 = ch * 1024 + cg * 512
                            kb.tt(dve, tm.t[:, :], ps.t[:, :], g2b.t[:, c0:c0 + 512], ALU.mult, [ps.s, g2b.s], [tm.s])
                            kb.tt(pool, X.t[:, cg * 512:(cg + 1) * 512], X.t[:, cg * 512:(cg + 1) * 512], tm.t[:, :],
                                  ALU.add, [X.s, tm.s], [X.s])
                        if dst_is_out:
                            kb.dma(act, xdst[t0 - NCTX:t0 - NCTX + 128, ch * 1024:(ch + 1) * 1024], X.t[:, :], [X.s], ())
                        else:
                            kb.dma(act, xdst[t0:t0 + 128, ch * 1024:(ch + 1) * 1024], X.t[:, :], [X.s], ())
                kb.barrier()

        def run_all():
            for l in range(n_layers):
                last = (l == DEPTH - 1)
                seqs = "x" if last else "cx"
                xsrc = xin if l == 0 else xl0
                p0_mod(l)
                if stop_after == f"p0_{l}":
                    return
                with ExitStack() as st:
                    hT = make_hT(st, l, xsrc, norm1_g, 0, D, "cx")
                    if stop_after == f"p1a_{l}":
                        return
                    sig = lambda ch: AF.Sigmoid if ch >= ZDG // 128 and ch < ZKR // 128 else None
                    ctx_chunks = set(range(ZKV // 128, ZP // 128)) | {ZKR // 128}
                    proj_fm(l, hT, w_in[l], NIN, zx, seqs, sig, ctx_chunks)
                if stop_after == f"p1_{l}":
                    return
                p2acd_fused(l, seqs)
                if stop_after == f"p2acd_{l}":
                    return
                p2b1_attnprep(l, seqs)
                if stop_after == f"p2b1_{l}":
                    return
                p2b2_attn(l, seqs)
                if stop_after == f"p2b2_{l}":
                    return
                p2m_merge(l, seqs, xsrc, xmid)
                if stop_after == f"p2m_{l}":
                    return
                with ExitStack() as st:
                    hT = make_hT(st, l, xmid, norm2_g, 3 * D, 4 * D, seqs)
                    proj_fm(l, hT, w_up[l], 2 * DFF, ffT, seqs, lambda ch: None)
                if stop_after == f"p3b_{l}":
                    return
                p3c_ffconv(l, seqs)
                if stop_after == f"p3c_{l}":
                    return
                p3d_down(l, seqs, xmid, out if last else xl0, last)

        run_all()
        kb.barrier()
    return nc


def _rope_tables():
    nf = 16
    inv = (10000.0 ** (-np.arange(nf, dtype=np.float32) / nf)).astype(np.float32)
    t = np.arange(NX)
    pos = np.stack([t // 64, t % 64], axis=0).astype(np.float32)
    C = np.zeros((128, NX), np.float32)
    S = np.zeros((128, NX), np.float32)
    for r in range(64):
        axis, f = r // 32, r % 16
        ang = pos[axis] * inv[f]
        C[r] = np.cos(ang)
        S[r] = np.sin(ang)
    C[64:] = C[:64]
    S[64:] = S[:64]
    Pm = np.zeros((128, 128), np.float32)
    for hb in (0, 64):
        for a in range(2):
            for f in range(16):
                r1 = hb + a * 32 + f
                r2 = r1 + 16
                Pm[r1, r2] = -1.0
                Pm[r2, r1] = 1.0
    return C, S, np.ascontiguousarray(Pm.T)


def _prep_shared(inp):
    f = lambda a: np.ascontiguousarray(np.asarray(a, dtype=np.float32))
    perm = np.concatenate([np.arange(0, 4352), np.arange(4416, NIN), np.arange(4352, 4416)])
    w_in = f(np.asarray(inp["w_in"])[:, :, perm])
    hq = np.arange(8)[:, None] * 192
    qperm = np.concatenate([(hq + np.arange(128)[None, :]).ravel(), (hq + 128 + np.arange(64)[None, :]).ravel()])
    w_qup = f(np.asarray(inp["w_q_up"])[:, :, qperm])
    hk = np.arange(8)[:, None] * 256
    kvperm = np.concatenate([(hk + np.arange(128)[None, :]).ravel(), (hk + 128 + np.arange(128)[None, :]).ravel()])
    w_kvup = f(np.asarray(inp["w_kv_up"])[:, :, kvperm])
    w_aout = f(np.stack([inp["w_a_out"], inp["w_mla_out"], inp["w_d_out"]], axis=1))
    w_pool = f(np.asarray(inp["w_pool"]).reshape(DEPTH, 1024, 512))
    sm = np.zeros((DEPTH, 128, NSMALL), np.float32)

    def fm(v, nch):
        return np.asarray(v, np.float32).reshape(nch, 128).T

    for l in range(DEPTH):
        ca = np.asarray(inp["conv_a_w"][l])
        sm[l, :, SP_CA:SP_CA + 24] = ca.T.reshape(8, 128, 3).transpose(1, 0, 2).reshape(128, 24)
        cd = np.asarray(inp["conv_d_w"][l])
        sm[l, :, SP_CD:SP_CD + 248] = cd.T.reshape(8, 128, 31).transpose(1, 0, 2).reshape(128, 248)
        sm[l, :, SP_CDB:SP_CDB + 8] = fm(inp["conv_d_b"][l], 8)
        sm[l, :, SP_LNG:SP_LNG + 8] = fm(inp["cd_ln_g"][l], 8)
        sm[l, :, SP_LNB:SP_LNB + 8] = fm(inp["cd_ln_b"][l], 8)
        sm[l, :, SP_PS:SP_PS + 16] = fm(inp["pool_scale"][l], 16)
        cf = np.asarray(inp["conv_ff_w"][l])
        sm[l, :, SP_CF:SP_CF + 132] = cf.T.reshape(NFC, 128, 3).transpose(1, 0, 2).reshape(128, 132)
        sm[l, :, SP_CFB:SP_CFB + NFC] = fm(inp["conv_ff_b"][l], NFC)
        sm[l, :, SP_QN:SP_QN + 6] = fm(inp["q_norm_g"][l], 6)
        sm[l, :, SP_KVN:SP_KVN + 4] = fm(inp["kv_norm_g"][l], 4)
        qh = np.asarray(inp["q_head_g"][l], np.float32)
        kh = np.asarray(inp["k_head_g"][l], np.float32)
        sm[l, :, SP_QHN] = qh[:128]
        sm[l, :, SP_QHR] = np.concatenate([qh[128:], qh[128:]])
        sm[l, :, SP_KHN] = kh[:128]
        sm[l, :, SP_KHR] = np.concatenate([kh[128:], kh[128:]])
    C, S, PmT = _rope_tables()
    return {
        "ada_w": f(inp["ada_w"]), "ada_b": f(inp["ada_b"]), "norm1_g": f(inp["norm1_g"]),
        "norm2_g": f(inp["norm2_g"]), "w_in": w_in, "w_aout": w_aout, "w_pool": w_pool, "w_out": f(inp["w_out"]),
        "w_qup": w_qup, "w_kvup": w_kvup, "w_up": f(inp["w_up"]), "w_down": f(inp["w_down"]), "small": sm,
        "rope_c": C, "rope_s": S, "pmat": PmT, "ident": np.eye(128, dtype=np.float32),
    }


def _prep_core(inp, b):
    xin = np.concatenate([np.asarray(inp["ctx"][b], np.float32), np.asarray(inp["x"][b], np.float32)], axis=0)
    cc = np.stack([np.asarray(inp["c"][b], np.float32), np.asarray(inp["c_ctx"], np.float32)], axis=-1)
    cc = np.ascontiguousarray(cc.reshape(KC, 128, 2).transpose(1, 0, 2))
    return {"xin": np.ascontiguousarray(xin), "cc": cc}


def kernel(**inputs):
    n = 8
    nc = build()
    shared = _prep_shared(inputs)
    in_maps = []
    for b in range(n):
        m = dict(shared)
        m.update(_prep_core(inputs, b))
        in_maps.append(m)
    res = run_bass_kernel_spmd(nc, in_maps, core_ids=list(range(n)))
    return np.stack([np.asarray(r["out"], dtype=np.float32) for r in res.results], axis=0)
```

```python
import math
from contextlib import ExitStack

import numpy as np
import concourse.bass as bass
import concourse.mybir as mybir
from concourse.bass_utils import run_bass_kernel_spmd

F32 = mybir.dt.float32
BF16 = mybir.dt.bfloat16
AF = mybir.ActivationFunctionType
ALU = mybir.AluOpType

D = 2048
KC = 16
NCTX = 256
NX = 4096
TA = NCTX + NX
NTT = TA // 128
DEPTH = 2
BLOCKS = [(0, 256)] + [(256 + 512 * i, 512) for i in range(8)]
EPS = 1e-6
LN_EPS = 1e-5
ZA, ZQ, ZKV, ZP, ZDA, ZDG, ZG, ZKR = 0, 3072, 3840, 4352, 5376, 6400, 7424, 15616
NIN = 15680
DFF = 5632
NFC = DFF // 128
SEM_LIMIT = 30000
PHASE_MARKS = []

SP_CA = 0
SP_CD = SP_CA + 24
SP_CDB = SP_CD + 248
SP_LNG = SP_CDB + 8
SP_LNB = SP_LNG + 8
SP_PS = SP_LNB + 8
SP_CF = SP_PS + 16
SP_CFB = SP_CF + 132
SP_QN = SP_CFB + 44
SP_KVN = SP_QN + 6
SP_QHN = SP_KVN + 4
SP_QHR = SP_QHN + 1
SP_KHN = SP_QHR + 1
SP_KHR = SP_KHN + 1
NSMALL = SP_KHR + 1


class Slot:
    __slots__ = ("w", "r", "ds", "psum")

    def __init__(self):
        self.w = None
        self.r = {}
        self.ds = None
        self.psum = False


class DSem:
    __slots__ = ("h", "cnt", "sw")

    def __init__(self, h):
        self.h = h
        self.cnt = 0
        self.sw = False


class Eng:
    def __init__(self, kb, name, h, compute=True):
        self.kb = kb
        self.name = name
        self.h = h
        self.sem = kb.nc.alloc_semaphore("e_" + name)
        self.cnt = 0
        self.seen = {}
        self.pend = False


class Tile:
    def __init__(self, t, nslots=1):
        self.t = t
        self.sl = [Slot() for _ in range(nslots)]

    @property
    def s(self):
        return self.sl[0]


class KB:
    def __init__(self, nc):
        self.nc = nc
        self.pe = Eng(self, "pe", nc.tensor)
        self.act = Eng(self, "act", nc.scalar)
        self.dve = Eng(self, "dve", nc.vector)
        self.pool = Eng(self, "pool", nc.gpsimd)
        self.sp = Eng(self, "sp", nc.sync)
        self.engs = [self.pe, self.act, self.dve, self.pool, self.sp]
        self.free_ds = []
        self.free_ds_sw = []
        self.live_ds = []
        self.nsem = 5
        self.tid = 0

    def tile(self, st, shape, dt, nslots=1, name=None):
        self.tid += 1
        t = st.enter_context(self.nc.sbuf_tensor(f"{name or 't'}_{self.tid}", list(shape), dt))
        return Tile(t, nslots)

    def get_ds(self, sw):
        fl = self.free_ds_sw if sw else self.free_ds
        if fl:
            ds = fl.pop()
        else:
            self.nsem += 1
            ds = DSem(self.nc.alloc_semaphore(f"d{self.nsem}"))
            ds.sw = sw
        self.live_ds.append(ds)
        return ds

    def _need(self, need, sv):
        if sv is None:
            return
        sem, val = sv
        if need.get(sem, 0) < val:
            need[sem] = val

    def _waits(self, eng, reads, writes):
        need = {}
        for s in reads:
            self._need(need, s.w)
            if s.psum:
                for sem, val in s.r.items():
                    if sem is not eng.sem:
                        need[sem] = max(need.get(sem, 0), val)
        for s in writes:
            self._need(need, s.w)
            for sem, val in s.r.items():
                need[sem] = max(need.get(sem, 0), val)
        for sem, val in need.items():
            if sem is eng.sem and eng is self.pe:
                continue
            if eng.seen.get(sem, 0) < val:
                eng.h.wait_ge(sem, val)
                eng.seen[sem] = val

    def op(self, eng, fn, reads=(), writes=(), inc=True):
        self._waits(eng, reads, writes)
        ins = fn(eng.h)
        val = eng.cnt + 1
        sem = eng.sem
        for s in reads:
            if s.r.get(sem, 0) < val:
                s.r[sem] = val
        for s in writes:
            s.w = (sem, val)
            s.r = {}
        if inc:
            ins.then_inc(sem, 1)
            eng.cnt = val
            eng.pend = False
            if eng.cnt >= SEM_LIMIT:
                self.nsem += 1
                eng.sem = self.nc.alloc_semaphore(f"e_{eng.name}_{self.nsem}")
                eng.cnt = 0
        else:
            eng.pend = True
        return ins

    def dma(self, q, out, in_, reads=(), writes=()):
        self._waits(q, reads, writes)
        owner = writes[0] if writes else reads[0]
        if owner.ds is None:
            owner.ds = {}
        sw = q is self.pool
        if sw not in owner.ds or owner.ds[sw].cnt + 16 > SEM_LIMIT:
            owner.ds[sw] = self.get_ds(sw)
        ds = owner.ds[sw]
        ds.cnt += 16
        q.h.dma_start(out=out, in_=in_).then_inc(ds.h, 16)
        for s in reads:
            if s.r.get(ds.h, 0) < ds.cnt:
                s.r[ds.h] = ds.cnt
        for s in writes:
            s.w = (ds.h, ds.cnt)
            s.r = {}

    def barrier(self):
        assert not self.pe.pend
        PHASE_MARKS.append(self.nc.n_instructions())
        for e in self.engs:
            for f in self.engs:
                if f is not e and f.cnt > 0 and e.seen.get(f.sem, 0) < f.cnt:
                    e.h.wait_ge(f.sem, f.cnt)
                    e.seen[f.sem] = f.cnt
            for ds in self.live_ds:
                if ds.cnt > 0 and e.seen.get(ds.h, 0) < ds.cnt:
                    e.h.wait_ge(ds.h, ds.cnt)
                    e.seen[ds.h] = ds.cnt
        for ds in self.live_ds:
            if ds.cnt + 2048 < SEM_LIMIT:
                (self.free_ds_sw if ds.sw else self.free_ds).append(ds)
        self.live_ds = []

    def mm(self, out, lhsT, rhs, start, stop, reads, writes, inc):
        return self.op(self.pe, lambda e: e.matmul(out, lhsT, rhs, start=start, stop=stop), reads, writes, inc)

    def actf(self, out, in_, func, reads, writes, bias=None, scale=None, accum_out=None):
        kw = {}
        if bias is not None:
            kw["bias"] = bias
        if scale is not None:
            kw["scale"] = scale
        if accum_out is not None:
            kw["accum_out"] = accum_out
        return self.op(self.act, lambda e: e.activation(out=out, in_=in_, func=func, **kw), reads, writes)

    def tt(self, eng, out, in0, in1, op, reads, writes):
        return self.op(eng, lambda e: e.tensor_tensor(out=out, in0=in0, in1=in1, op=op), reads, writes)

    def ts(self, eng, out, in0, s1, s2, op0, op1, reads, writes):
        if s2 is None:
            return self.op(eng, lambda e: e.tensor_scalar(out=out, in0=in0, scalar1=s1, scalar2=None, op0=op0),
                           reads, writes)
        return self.op(eng, lambda e: e.tensor_scalar(out=out, in0=in0, scalar1=s1, scalar2=s2, op0=op0, op1=op1),
                       reads, writes)

    def stt(self, eng, out, in0, scalar, in1, op0, op1, reads, writes):
        return self.op(eng, lambda e: e.scalar_tensor_tensor(out=out, in0=in0, scalar=scalar, in1=in1,
                                                             op0=op0, op1=op1), reads, writes)

    def copy(self, eng, out, in_, reads, writes):
        if eng is self.act:
            return self.actf(out, in_, AF.Copy, reads, writes)
        return self.op(eng, lambda e: e.tensor_copy(out=out, in_=in_), reads, writes)

    def memset(self, eng, ap, val, writes):
        return self.op(eng, lambda e: e.memset(ap, val), (), writes)


def poff(t, P):
    return P + t if t < NCTX else 2 * P + t


def build(n_layers=DEPTH, dbg=False, stop_after=None):
    nc = bass.Bass("TRN2", target_bir_lowering=False)
    kb = KB(nc)
    pe, act, dve, pool, sp = kb.pe, kb.act, kb.dve, kb.pool, kb.sp
    dbg = set(dbg) if dbg else set()

    def din(name, shape, dt=F32):
        return nc.dram_tensor(name, list(shape), dt, kind="ExternalInput").ap()

    def dscr(name, shape, dt=BF16):
        return nc.dram_tensor(name, list(shape), dt, kind=("ExternalOutput" if name in dbg else "Internal")).ap()

    xin = din("xin", [TA, D])
    cc = din("cc", [128, KC, 2])
    ada_w = din("ada_w", [DEPTH, D, 6 * D])
    ada_b = din("ada_b", [DEPTH, 6 * D])
    norm1_g = din("norm1_g", [DEPTH, D])
    norm2_g = din("norm2_g", [DEPTH, D])
    w_in = din("w_in", [DEPTH, D, NIN])
    w_aout = din("w_aout", [DEPTH, 3, 1024, D])
    w_pool = din("w_pool", [DEPTH, 1024, 512])
    w_out = din("w_out", [DEPTH, D, D])
    w_qup = din("w_qup", [DEPTH, 768, 1536])
    w_kvup = din("w_kvup", [DEPTH, 512, 2048])
    w_up = din("w_up", [DEPTH, D, 2 * DFF])
    w_down = din("w_down", [DEPTH, DFF, D])
    small = din("small", [DEPTH, 128, NSMALL])
    rope_c = din("rope_c", [128, NX])
    rope_s = din("rope_s", [128, NX])
    pmat = din("pmat", [128, 128])
    ident_in = din("ident", [128, 128])
    out = nc.dram_tensor("out", [NX, D], F32, kind="ExternalOutput").ap()

    modd = dscr("modd", [2, 6 * D], F32)
    zx = dscr("zx", [NIN, TA])
    qT = dscr("qT", [1536, TA])
    kT = dscr("kT", [1536, TA])
    vv = dscr("vv", [TA, 1024])
    aT = dscr("aT", [4, 1024, TA])
    xmid = dscr("xmid", [TA, D], F32)
    xl0 = dscr("xl0", [TA, D], F32)
    ffT = dscr("ffT", [2 * DFF, TA])
    fT = dscr("fT", [DFF, TA])

    psum = [Tile(nc.alloc_psum_tensor(f"ps{i}", [128, 512], F32)) for i in range(8)]
    for p_ in psum:
        p_.s.psum = True
    psn = [0]

    def next_ps():
        p = psum[psn[0] % 8]
        psn[0] += 1
        return p

    with ExitStack() as gst, nc.Block() as block:
        ident = kb.tile(gst, [128, 128], BF16, name="ident")
        ones = kb.tile(gst, [128, 128], BF16, name="ones")
        pm = kb.tile(gst, [128, 128], BF16, name="pm")
        ctmp = kb.tile(gst, [128, 128], F32, name="ctmp")
        kb.dma(sp, ctmp.t[:, :], ident_in, (), [ctmp.s])
        kb.copy(dve, ident.t[:, :], ctmp.t[:, :], [ctmp.s], [ident.s])
        kb.dma(sp, ctmp.t[:, :], pmat, (), [ctmp.s])
        kb.copy(dve, pm.t[:, :], ctmp.t[:, :], [ctmp.s], [pm.s])
        kb.memset(dve, ones.t[:, :], 1.0, [ones.s])
        epsb = kb.tile(gst, [128, 1], F32, name="epsb")
        lnepsb = kb.tile(gst, [128, 1], F32, name="lnepsb")
        kb.memset(dve, epsb.t[:, :], EPS, [epsb.s])
        kb.memset(dve, lnepsb.t[:, :], LN_EPS, [lnepsb.s])

        def phase_done(name):
            kb.barrier()
            return stop_after == name

        def p0_mod(l):
            with ExitStack() as st:
                cs = kb.tile(st, [128, KC, 2], F32)
                cb = kb.tile(st, [128, KC, 2], BF16)
                ab = kb.tile(st, [2, 6 * D], F32)
                mo = kb.tile(st, [2, 6 * D], F32)
                wt = [kb.tile(st, [128, KC, 512], BF16) for _ in range(3)]
                kb.dma(sp, cs.t[:, :, :], cc, (), [cs.s])
                kb.actf(cb.t[:, :, :], cs.t[:, :, :], AF.Silu, [cs.s], [cb.s])
                kb.dma(sp, ab.t[:, :], ada_b[l:l + 1, :].partition_broadcast(2), (), [ab.s])
                wv = ada_w[l].rearrange("(kc p) n -> p kc n", p=128)
                for g in range(24):
                    w = wt[g % 3]
                    kb.dma(pool, w.t[:, :, :], wv[:, :, g * 512:(g + 1) * 512], (), [w.s])
                    ps = next_ps()
                    for kc in range(KC):
                        kb.mm(ps.t[0:2, :], cb.t[:, kc, :], w.t[:, kc, :], kc == 0, kc == KC - 1,
                              [cb.s, w.s], [ps.s], kc == KC - 1)
                    kb.tt(dve, mo.t[:, g * 512:(g + 1) * 512], ps.t[0:2, :], ab.t[:, g * 512:(g + 1) * 512],
                          ALU.add, [ps.s, ab.s], [mo.s])
                kb.dma(sp, modd, mo.t[:, :], [mo.s], ())
                kb.barrier()

        def make_hT(st_outer, l, src, norm_g, sh_off, sc_off, seqs):
            hT = kb.tile(st_outer, [128, KC, TA], BF16, nslots=NTT, name="hT")
            with ExitStack() as st:
                xt = [kb.tile(st, [128, D], F32) for _ in range(3)]
                hb = [kb.tile(st, [128, D], BF16) for _ in range(3)]
                gsc = kb.tile(st, [128, D], F32)
                gtm = kb.tile(st, [128, D], F32)
                shb = kb.tile(st, [128, D], F32)
                ss = [kb.tile(st, [128, 1], F32) for _ in range(3)]
                rs = [kb.tile(st, [128, 1], F32) for _ in range(3)]
                cur = [None]
                tiles = [tt for tt in range(NTT) if tt >= 2 or "c" in seqs]

                def stage_a(tt):
                    which = 1 if tt < 2 else 0
                    if which != cur[0]:
                        cur[0] = which
                        kb.dma(sp, gsc.t[:, :], modd[which:which + 1, sc_off:sc_off + D].partition_broadcast(128),
                               (), [gsc.s])
                        kb.dma(sp, gtm.t[:, :], norm_g[l:l + 1, :].partition_broadcast(128), (), [gtm.s])
                        kb.dma(sp, shb.t[:, :], modd[which:which + 1, sh_off:sh_off + D].partition_broadcast(128),
                               (), [shb.s])
                        kb.stt(dve, gsc.t[:, :], gsc.t[:, :], 1.0, gtm.t[:, :], ALU.add, ALU.mult,
                               [gsc.s, gtm.s], [gsc.s])
                    x = xt[tt % 3]
                    h = hb[tt % 3]
                    s_ = ss[tt % 3]
                    r_ = rs[tt % 3]
                    t0 = tt * 128
                    kb.dma(sp, x.t[:, :], src[t0:t0 + 128, :], (), [x.s])
                    kb.memset(dve, s_.t[:, :], 0.0, [s_.s])
                    kb.actf(h.t[:, :], x.t[:, :], AF.Square, [x.s], [h.s, s_.s], accum_out=s_.t[:, 0:1])
                    kb.actf(r_.t[:, :], s_.t[:, :], AF.Sqrt, [s_.s], [r_.s], bias=epsb.t[:, 0:1], scale=1.0 / D)
                    kb.op(dve, lambda e, r_=r_: e.reciprocal(out=r_.t[:, :], in_=r_.t[:, :]), [r_.s], [r_.s])
                    kb.stt(dve, x.t[:, :], x.t[:, :], r_.t[:, 0:1], gsc.t[:, :], ALU.mult, ALU.mult,
                           [x.s, r_.s, gsc.s], [x.s])
                    kb.tt(pool, h.t[:, :], x.t[:, :], shb.t[:, :], ALU.add, [x.s, shb.s], [h.s])

                def stage_b(tt):
                    h = hb[tt % 3]
                    t0 = tt * 128
                    pa = next_ps()
                    pb = next_ps()
                    for half, p in enumerate((pa, pb)):
                        pv = p.t[:, :].bitcast(BF16)
                        for j in range(8):
                            kcx = half * 8 + j
                            kb.op(pe, lambda e, pv=pv, j=j, kcx=kcx: e.transpose(
                                pv[:, j * 128:(j + 1) * 128], h.t[:, kcx * 128:(kcx + 1) * 128], ident.t[:, :]),
                                [h.s, ident.s], [p.s], inc=(j == 7))
                        eng = act if half == 0 else dve
                        kb.copy(eng, hT.t[:, half * 8:half * 8 + 8, t0:t0 + 128],
                                pv[:, 0:1024].rearrange("p (k t) -> p k t", k=8), [p.s], [hT.sl[tt]])

                for i, tt in enumerate(tiles):
                    if i == 0:
                        stage_a(tt)
                    if i + 1 < len(tiles):
                        stage_a(tiles[i + 1])
                    stage_b(tt)
                kb.barrier()
            return hT

        def hT_slots(hT, t0, n):
            return [hT.sl[i] for i in range(t0 // 128, (t0 + n) // 128)]

        def proj_fm(l, hT, wsrc, ncols, dst, seqs, func_of_chunk, ctx_chunks=None):
            with ExitStack() as st:
                wt = [kb.tile(st, [128, KC, 512], BF16) for _ in range(2)]
                orow = [kb.tile(st, [128, TA], BF16, nslots=len(BLOCKS)) for _ in range(3)]
                wv = wsrc.rearrange("(kc p) n -> p kc n", p=128)
                ngroups = (ncols + 511) // 512
                ci = 0
                for g in range(ngroups):
                    c0 = g * 512
                    gw = min(512, ncols - c0)
                    w = wt[g % 2]
                    kb.dma(pool, w.t[:, :, 0:gw], wv[:, :, c0:c0 + gw], (), [w.s])
                    for cj in range((gw + 127) // 128):
                        m = min(128, gw - cj * 128)
                        chunk = (c0 + cj * 128) // 128
                        func = func_of_chunk(chunk)
                        orw = orow[ci % 3]
                        ci += 1
                        used = []
                        for bi, (t0, n) in enumerate(BLOCKS):
                            if bi == 0:
                                if "c" not in seqs and (ctx_chunks is None or chunk not in ctx_chunks):
                                    continue
                            used.append(bi)
                            ps = next_ps()
                            hs = hT_slots(hT, t0, n)
                            for kc in range(KC):
                                kb.mm(ps.t[0:m, 0:n], w.t[:, kc, cj * 128:cj * 128 + m], hT.t[:, kc, t0:t0 + n],
                                      kc == 0, kc == KC - 1, [w.s] + hs, [ps.s], kc == KC - 1)
                            if func is None:
                                eng = dve if (bi % 2 == 0) else act
                                kb.copy(eng, orw.t[0:m, t0:t0 + n], ps.t[0:m, 0:n], [ps.s], [orw.sl[bi]])
                            else:
                                kb.actf(orw.t[0:m, t0:t0 + n], ps.t[0:m, 0:n], func, [ps.s], [orw.sl[bi]])
                        ta = BLOCKS[used[0]][0]
                        r0 = c0 + cj * 128
                        kb.dma(sp, dst[r0:r0 + m, ta:TA], orw.t[0:m, ta:TA], [orw.sl[b] for b in used], ())
                kb.barrier()

        def load_small(st, l):
            sm = kb.tile(st, [128, NSMALL], F32, name="small")
            kb.dma(sp, sm.t[:, :], small[l], (), [sm.s])
            return sm

        def row_load(q, dstt, P, src_rows, seqs, slot):
            if "c" in seqs:
                kb.dma(q, dstt.t[:, P:P + NCTX], src_rows[:, 0:NCTX], (), [slot])
            kb.dma(q, dstt.t[:, 2 * P + NCTX:2 * P + TA], src_rows[:, NCTX:TA], (), [slot])

        def make_diags(eng, dg, sm, col0, ntap, stride=1):
            for k in range(ntap):
                kb.ts(eng, dg.t[:, k, :], ident.t[:, :], sm.t[:, col0 + k * stride:col0 + k * stride + 1], None,
                      ALU.mult, None, [ident.s, sm.s], [dg.s])

        def seq_blocks(seqs):
            return [(bi, t0, n) for bi, (t0, n) in enumerate(BLOCKS) if bi > 0 or "c" in seqs]

        def p2a_shortconv(l, seqs):
            P = 1
            W = TA + 3 * P
            with ExitStack() as st:
                sm = load_small(st, l)
                brow = [kb.tile(st, [128, TA], BF16) for _ in range(2)]
                crow = [kb.tile(st, [128, W], BF16) for _ in range(2)]
                hrow = [kb.tile(st, [128, W], BF16) for _ in range(2)]
                orow = [kb.tile(st, [128, TA], BF16, nslots=len(BLOCKS)) for _ in range(2)]
                dgs = [kb.tile(st, [128, 3, 128], BF16) for _ in range(2)]
                for t in crow + hrow:
                    kb.memset(pool, t.t[:, :], 0.0, [t.s])
                ta = 0 if "c" in seqs else NCTX
                for c in range(8):
                    b, cg, hh, orw, dg = brow[c % 2], crow[c % 2], hrow[c % 2], orow[c % 2], dgs[c % 2]
                    kb.dma(sp, b.t[:, ta:TA], zx[ZA + c * 128:ZA + (c + 1) * 128, ta:TA], (), [b.s])
                    row_load(sp, cg, P, zx[ZA + 1024 + c * 128:ZA + 1024 + (c + 1) * 128, :], seqs, cg.s)
                    row_load(sp, hh, P, zx[ZA + 2048 + c * 128:ZA + 2048 + (c + 1) * 128, :], seqs, hh.s)
                    kb.tt(pool, cg.t[:, :], cg.t[:, :], hh.t[:, :], ALU.mult, [cg.s, hh.s], [cg.s])
                    make_diags(dve, dg, sm, SP_CA + c * 3, 3)
                    for bi, t0, n in seq_blocks(seqs):
                        ps = next_ps()
                        p0 = poff(t0, P) - P
                        for k in range(3):
                            kb.mm(ps.t[:, 0:n], dg.t[:, k, :], cg.t[:, p0 + k:p0 + k + n], k == 0, k == 2,
                                  [dg.s, cg.s], [ps.s], k == 2)
                        kb.tt(dve, orw.t[:, t0:t0 + n], ps.t[:, 0:n], b.t[:, t0:t0 + n], ALU.mult,
                              [ps.s, b.s], [orw.sl[bi]])
                    kb.dma(sp, aT[0, c * 128:(c + 1) * 128, ta:TA], orw.t[:, ta:TA],
                           [orw.sl[bi] for bi, _, _ in seq_blocks(seqs)], ())
                kb.barrier()

        def p2d_confconv(l, seqs):
            P = 15
            W = TA + 3 * P
            with ExitStack() as st:
                sm = load_small(st, l)
                arow = [kb.tile(st, [128, W], BF16) for _ in range(2)]
                grow = [kb.tile(st, [128, W], BF16) for _ in range(2)]
                orow = [kb.tile(st, [128, TA], BF16, nslots=len(BLOCKS)) for _ in range(2)]
                dgs = [kb.tile(st, [128, 31, 128], BF16) for _ in range(2)]
                for t in arow + grow:
                    kb.memset(pool, t.t[:, :], 0.0, [t.s])
                ta = 0 if "c" in seqs else NCTX
                for c in range(8):
                    a, g, orw, dg = arow[c % 2], grow[c % 2], orow[c % 2], dgs[c % 2]
                    row_load(sp, a, P, zx[ZDA + c * 128:ZDA + (c + 1) * 128, :], seqs, a.s)
                    row_load(sp, g, P, zx[ZDG + c * 128:ZDG + (c + 1) * 128, :], seqs, g.s)
                    kb.tt(dve, a.t[:, :], a.t[:, :], g.t[:, :], ALU.mult, [a.s, g.s], [a.s])
                    make_diags(dve, dg, sm, SP_CD + c * 31, 31)
                    for bi, t0, n in seq_blocks(seqs):
                        ps = next_ps()
                        p0 = poff(t0, P) - P
                        for k in range(31):
                            kb.mm(ps.t[:, 0:n], dg.t[:, k, :], a.t[:, p0 + k:p0 + k + n], k == 0, k == 30,
                                  [dg.s, a.s], [ps.s], k == 30)
                        kb.actf(orw.t[:, t0:t0 + n], ps.t[:, 0:n], AF.Identity, [ps.s, sm.s], [orw.sl[bi]],
                                bias=sm.t[:, SP_CDB + c:SP_CDB + c + 1])
                    kb.dma(sp, aT[3, c * 128:(c + 1) * 128, ta:TA], orw.t[:, ta:TA],
                           [orw.sl[bi] for bi, _, _ in seq_blocks(seqs)], ())
                kb.barrier()

        def p2acd_fused(l, seqs):
            PA, PC, PD = 1, 16, 15
            WA_, WC_, WD_ = TA + 3 * PA, TA + 3 * PC, TA + 3 * PD
            blocks = seq_blocks(seqs)
            bis = [bi for bi, _, _ in blocks]
            ta = 0 if "c" in seqs else NCTX
            with ExitStack() as st:
                sm = load_small(st, l)
                arow = [kb.tile(st, [128, WD_], BF16) for _ in range(2)]
                grow = [kb.tile(st, [128, WD_], BF16) for _ in range(2)]
                orowd = [kb.tile(st, [128, TA], BF16, nslots=len(BLOCKS)) for _ in range(2)]
                dgsd = [kb.tile(st, [128, 31, 128], BF16) for _ in range(2)]
                brow = kb.tile(st, [128, TA], BF16)
                crow = kb.tile(st, [128, WA_], BF16)
                hrow = kb.tile(st, [128, WA_], BF16)
                orowa = kb.tile(st, [128, TA], BF16, nslots=len(BLOCKS))
                dga = kb.tile(st, [128, 3, 128], BF16)
                urow = kb.tile(st, [128, WC_], BF16)
                sa = kb.tile(st, [128, WC_], F32)
                sb = kb.tile(st, [128, WC_], F32)
                inv = kb.tile(st, [128, WC_], F32)
                orowc = kb.tile(st, [128, TA], BF16)
                for t in arow + grow + [crow, hrow, urow, sa, sb]:
                    kb.memset(pool, t.t[:, :], 0.0, [t.s])

                def d_prep(c):
                    a_, g_, dg = arow[c % 2], grow[c % 2], dgsd[c % 2]
                    row_load(sp, a_, PD, zx[ZDA + c * 128:ZDA + (c + 1) * 128, :], seqs, a_.s)
                    row_load(sp, g_, PD, zx[ZDG + c * 128:ZDG + (c + 1) * 128, :], seqs, g_.s)
                    kb.tt(dve, a_.t[:, :], a_.t[:, :], g_.t[:, :], ALU.mult, [a_.s, g_.s], [a_.s])
                    make_diags(dve, dg, sm, SP_CD + c * 31, 31)

                def d_mm(c):
                    a_, orw, dg = arow[c % 2], orowd[c % 2], dgsd[c % 2]
                    for bi, t0, n in blocks:
                        ps = next_ps()
                        p0 = poff(t0, PD) - PD
                        for k in range(31):
                            kb.mm(ps.t[:, 0:n], dg.t[:, k, :], a_.t[:, p0 + k:p0 + k + n], k == 0, k == 30,
                                  [dg.s, a_.s], [ps.s], k == 30)
                        kb.actf(orw.t[:, t0:t0 + n], ps.t[:, 0:n], AF.Identity, [ps.s, sm.s], [orw.sl[bi]],
                                bias=sm.t[:, SP_CDB + c:SP_CDB + c + 1])
                    kb.dma(pool, aT[3, c * 128:(c + 1) * 128, ta:TA], orw.t[:, ta:TA], [orw.sl[b] for b in bis], ())

                def a_prep(c):
                    kb.dma(sp, brow.t[:, ta:TA], zx[ZA + c * 128:ZA + (c + 1) * 128, ta:TA], (), [brow.s])
                    row_load(sp, crow, PA, zx[ZA + 1024 + c * 128:ZA + 1024 + (c + 1) * 128, :], seqs, crow.s)
                    row_load(sp, hrow, PA, zx[ZA + 2048 + c * 128:ZA + 2048 + (c + 1) * 128, :], seqs, hrow.s)
                    kb.tt(pool, crow.t[:, :], crow.t[:, :], hrow.t[:, :], ALU.mult, [crow.s, hrow.s], [crow.s])
                    make_diags(dve, dga, sm, SP_CA + c * 3, 3)

                def a_mm(c):
                    for bi, t0, n in blocks:
                        ps = next_ps()
                        p0 = poff(t0, PA) - PA
                        for k in range(3):
                            kb.mm(ps.t[:, 0:n], dga.t[:, k, :], crow.t[:, p0 + k:p0 + k + n], k == 0, k == 2,
                                  [dga.s, crow.s], [ps.s], k == 2)
                        kb.tt(dve, orowa.t[:, t0:t0 + n], ps.t[:, 0:n], brow.t[:, t0:t0 + n], ALU.mult,
                              [ps.s, brow.s], [orowa.sl[bi]])
                    kb.dma(pool, aT[0, c * 128:(c + 1) * 128, ta:TA], orowa.t[:, ta:TA],
                           [orowa.sl[b] for b in bis], ())

                W = WC_
                P = PC
                tp = poff(ta, P)

                def doubling(src, A, B, nst):
                    kb.tt(dve, A.t[:, 1:W], src.t[:, 0:W - 1], src.t[:, 1:W], ALU.add, [src.s], [A.s])
                    cur, oth = A, B
                    lo, hi = 1, W
                    for sh in (1, 2, 4)[:nst - 1]:
                        nlo, nhi = lo + sh, hi - sh
                        kb.tt(dve, oth.t[:, nlo:nhi], cur.t[:, nlo - sh:nhi - sh], cur.t[:, nlo + sh:nhi + sh],
                              ALU.add, [cur.s], [oth.s])
                        cur, oth = oth, cur
                        lo, hi = nlo, nhi
                    return cur

                def c_all(c):
                    g = c // 2
                    if c % 2 == 0:
                        kb.memset(dve, sa.t[:, :], 0.0, [sa.s])
                        if "c" in seqs:
                            kb.memset(dve, sa.t[:, P:P + NCTX], 1.0, [sa.s])
                        kb.memset(dve, sa.t[:, 2 * P + NCTX:2 * P + TA], 1.0, [sa.s])
                        kb.memset(dve, sb.t[:, :], 0.0, [sb.s])
                        kb.memset(dve, inv.t[:, :], 0.0, [inv.s])
                        cnt = doubling(sa, inv, sb, g + 1)
                        kb.ts(dve, cnt.t[:, :], cnt.t[:, :], 1.0, None, ALU.max, None, [cnt.s], [cnt.s])
                        kb.actf(inv.t[:, :], cnt.t[:, :], AF.Ln, [cnt.s], [inv.s])
                        kb.actf(inv.t[:, :], inv.t[:, :], AF.Exp, [inv.s], [inv.s], scale=-1.0)
                        kb.memset(dve, sa.t[:, :], 0.0, [sa.s])
                        kb.memset(dve, sb.t[:, :], 0.0, [sb.s])
                    row_load(sp, urow, P, zx[ZP + c * 128:ZP + (c + 1) * 128, :], seqs, urow.s)
                    s_ = doubling(urow, sa, sb, g + 1)
                    kb.tt(dve, s_.t[:, tp:2 * P + TA], s_.t[:, tp:2 * P + TA], inv.t[:, tp:2 * P + TA], ALU.mult,
                          [s_.s, inv.s], [s_.s])
                    if "c" in seqs:
                        kb.tt(dve, orowc.t[:, 0:NCTX], s_.t[:, P:P + NCTX], urow.t[:, P:P + NCTX], ALU.subtract,
                              [s_.s, urow.s], [orowc.s])
                    kb.tt(dve, orowc.t[:, NCTX:TA], s_.t[:, 2 * P + NCTX:2 * P + TA],
                          urow.t[:, 2 * P + NCTX:2 * P + TA], ALU.subtract, [s_.s, urow.s], [orowc.s])
                    kb.dma(pool, aT[2, c * 128:(c + 1) * 128, ta:TA], orowc.t[:, ta:TA], [orowc.s], ())

                d_prep(0)
                for c in range(8):
                    c_all(c)
                    if c + 1 < 8:
                        d_prep(c + 1)
                    a_prep(c)
                    d_mm(c)
                    a_mm(c)
                kb.barrier()

        def p2c_pool(l, seqs):
            P = 16
            W = TA + 3 * P
            with ExitStack() as st:
                urow = [kb.tile(st, [128, W], BF16) for _ in range(2)]
                sa = [kb.tile(st, [128, W], F32) for _ in range(2)]
                sb = [kb.tile(st, [128, W], F32) for _ in range(2)]
                inv = kb.tile(st, [128, W], F32)
                ca = kb.tile(st, [128, W], F32)
                cb = kb.tile(st, [128, W], F32)
                orow = [kb.tile(st, [128, TA], BF16) for _ in range(2)]
                for t in urow + sa + sb + [ca, cb]:
                    kb.memset(pool, t.t[:, :], 0.0, [t.s])
                ta = 0 if "c" in seqs else NCTX
                tp = poff(ta, P)

                def doubling(eng, src, A, B, nst):
                    kb.tt(eng, A.t[:, 1:W], src.t[:, 0:W - 1], src.t[:, 1:W], ALU.add, [src.s], [A.s])
                    cur, oth = A, B
                    lo, hi = 1, W
                    for sh in (1, 2, 4)[:nst - 1]:
                        nlo, nhi = lo + sh, hi - sh
                        kb.tt(eng, oth.t[:, nlo:nhi], cur.t[:, nlo - sh:nhi - sh], cur.t[:, nlo + sh:nhi + sh],
                              ALU.add, [cur.s], [oth.s])
                        cur, oth = oth, cur
                        lo, hi = nlo, nhi
                    return cur

                for g in range(4):
                    kb.memset(dve, ca.t[:, :], 0.0, [ca.s])
                    if "c" in seqs:
                        kb.memset(dve, ca.t[:, P:P + NCTX], 1.0, [ca.s])
                    kb.memset(dve, ca.t[:, 2 * P + NCTX:2 * P + TA], 1.0, [ca.s])
                    kb.memset(dve, cb.t[:, :], 0.0, [cb.s])
                    kb.memset(dve, inv.t[:, :], 0.0, [inv.s])
                    cnt = doubling(dve, ca, inv, cb, g + 1)
                    kb.ts(dve, cnt.t[:, :], cnt.t[:, :], 1.0, None, ALU.max, None, [cnt.s], [cnt.s])
                    if cnt is not inv:
                        kb.op(dve, lambda e: e.reciprocal(out=inv.t[:, :], in_=cnt.t[:, :]), [cnt.s], [inv.s])
                    else:
                        kb.op(dve, lambda e: e.reciprocal(out=inv.t[:, :], in_=inv.t[:, :]), [inv.s], [inv.s])
                    for cc_ in range(2):
                        c = g * 2 + cc_
                        eng = dve if cc_ == 0 else pool
                        u, A, B, orw = urow[cc_], sa[cc_], sb[cc_], orow[cc_]
                        row_load(sp, u, P, zx[ZP + c * 128:ZP + (c + 1) * 128, :], seqs, u.s)
                        s = doubling(eng, u, A, B, g + 1)
                        kb.tt(eng, s.t[:, tp:2 * P + TA], s.t[:, tp:2 * P + TA], inv.t[:, tp:2 * P + TA], ALU.mult,
                              [s.s, inv.s], [s.s])
                        if "c" in seqs:
                            kb.tt(eng, orw.t[:, 0:NCTX], s.t[:, P:P + NCTX], u.t[:, P:P + NCTX], ALU.subtract,
                                  [s.s, u.s], [orw.s])
                        kb.tt(eng, orw.t[:, NCTX:TA], s.t[:, 2 * P + NCTX:2 * P + TA], u.t[:, 2 * P + NCTX:2 * P + TA],
                              ALU.subtract, [s.s, u.s], [orw.s])
                        kb.dma(sp, aT[2, c * 128:(c + 1) * 128, ta:TA], orw.t[:, ta:TA], [orw.s], ())
                kb.barrier()

        def rstd_from_ps(eng, dst, ps, n, inv_dim, eps, dslot, pslot):
            eb = epsb if eps == EPS else lnepsb
            kb.actf(dst.t[:, 0:n], ps.t[:, 0:n], AF.Ln, [pslot], [dslot], bias=eb.t[:, 0:1], scale=inv_dim)
            kb.actf(dst.t[:, 0:n], dst.t[:, 0:n], AF.Exp, [dslot], [dslot], scale=-0.5)

        def p2b1_attnprep(l, seqs):
            with ExitStack() as st:
                sm = load_small(st, l)
                wq = kb.tile(st, [128, 6, 1536], BF16)
                wkv = kb.tile(st, [128, 4, 2048], BF16)
                kb.dma(pool, wq.t[:, :, :], w_qup[l].rearrange("(kc p) n -> p kc n", p=128), (), [wq.s])
                kb.dma(pool, wkv.t[:, :, :], w_kvup[l].rearrange("(kc p) n -> p kc n", p=128), (), [wkv.s])
                zq = [kb.tile(st, [128, 6, 512], BF16) for _ in range(2)]
                zkv = [kb.tile(st, [128, 4, 512], BF16) for _ in range(2)]
                zkr = [kb.tile(st, [128, 512], BF16) for _ in range(2)]
                rc = [kb.tile(st, [128, 512], F32) for _ in range(2)]
                rsn = [kb.tile(st, [128, 512], F32) for _ in range(2)]

                class TS:
                    pass

                def mk(nsq, nzn):
                    T = TS()
                    T.sq = kb.tile(st, [128, nsq, 512], BF16, nslots=nsq)
                    T.raw = kb.tile(st, [128, nsq, 512], BF16, nslots=nsq)
                    T.zn = kb.tile(st, [128, nzn, 512], BF16)
                    T.rstd = kb.tile(st, [128, 512], F32)
                    T.rh = [kb.tile(st, [128, 512], F32) for _ in range(2)]
                    T.t1 = kb.tile(st, [128, 512], F32)
                    T.t2 = kb.tile(st, [128, 512], F32)
                    T.rr = kb.tile(st, [128, 512], BF16)
                    T.qo = [kb.tile(st, [128, 512], BF16) for _ in range(4)]
                    T.qoi = 0
                    return T

                TQ = mk(12, 6)
                TK = mk(9, 4)
                rrot = kb.tile(st, [128, 512], F32)
                vo = [kb.tile(st, [128, 1024], BF16) for _ in range(2)]

                def next_qo(T):
                    q = T.qo[T.qoi % 4]
                    T.qoi += 1
                    return q

                def loads(bi):
                    t0, n = BLOCKS[bi]
                    isx = bi > 0
                    do_q = isx or ("c" in seqs)
                    b2 = bi % 2
                    Zq, Zkv, Zkr, Rc, Rs = zq[b2], zkv[b2], zkr[b2], rc[b2], rsn[b2]
                    if do_q:
                        kb.dma(sp, Zq.t[:, :, 0:n], zx[ZQ:ZQ + 768, t0:t0 + n].rearrange("(c p) t -> p c t", p=128),
                               (), [Zq.s])
                    kb.dma(sp, Zkv.t[:, :, 0:n], zx[ZKV:ZKV + 512, t0:t0 + n].rearrange("(c p) t -> p c t", p=128),
                           (), [Zkv.s])
                    kb.dma(sp, Zkr.t[0:64, 0:n], zx[ZKR:ZKR + 64, t0:t0 + n], (), [Zkr.s])
                    kb.dma(sp, Zkr.t[64:128, 0:n], zx[ZKR:ZKR + 64, t0:t0 + n], (), [Zkr.s])
                    if isx:
                        kb.dma(sp, Rc.t[:, 0:n], rope_c[:, t0 - NCTX:t0 - NCTX + n], (), [Rc.s])
                        kb.dma(sp, Rs.t[:, 0:n], rope_s[:, t0 - NCTX:t0 - NCTX + n], (), [Rs.s])

                def lowrank_norm(T, Z, nch, inv_dim, gcol, n):
                    for c in range(nch):
                        kb.actf(T.sq.t[:, c, 0:n], Z.t[:, c, 0:n], AF.Square, [Z.s], [T.sq.sl[c]])
                    ps = next_ps()
                    for c in range(nch):
                        kb.mm(ps.t[:, 0:n], ones.t[:, :], T.sq.t[:, c, 0:n], c == 0, c == nch - 1,
                              [ones.s, T.sq.sl[c]], [ps.s], c == nch - 1)
                    rstd_from_ps(dve, T.rstd, ps, n, inv_dim, EPS, T.rstd.s, ps.s)
                    for c in range(nch):
                        kb.stt(dve, T.zn.t[:, c, 0:n], Z.t[:, c, 0:n], sm.t[:, gcol + c:gcol + c + 1],
                               T.rstd.t[:, 0:n], ALU.mult, ALU.mult, [Z.s, sm.s, T.rstd.s], [T.zn.s])

                def rope_rot(T, Rc, Rs, srcT, srcslot, dstT, dstslot, n):
                    ps = next_ps()
                    kb.mm(ps.t[:, 0:n], pm.t[:, :], srcT[:, 0:n], True, True, [pm.s, srcslot], [ps.s], True)
                    a, b_ = T.t1, T.t2
                    kb.tt(dve, a.t[:, 0:n], srcT[:, 0:n], Rc.t[:, 0:n], ALU.mult, [srcslot, Rc.s], [a.s])
                    kb.tt(dve, b_.t[:, 0:n], ps.t[:, 0:n], Rs.t[:, 0:n], ALU.mult, [ps.s, Rs.s], [b_.s])
                    kb.tt(pool, dstT[:, 0:n], a.t[:, 0:n], b_.t[:, 0:n], ALU.add, [a.s, b_.s], [dstslot])

                def q_path(bi):
                    t0, n = BLOCKS[bi]
                    isx = bi > 0
                    b2 = bi % 2
                    T = TQ
                    Zq, Rc, Rs = zq[b2], rc[b2], rsn[b2]
                    lowrank_norm(T, Zq, 6, 1.0 / 768, SP_QN, n)
                    yield
                    for oc in range(12):
                        ps = next_ps()
                        for kc in range(6):
                            kb.mm(ps.t[:, 0:n], wq.t[:, kc, oc * 128:(oc + 1) * 128], T.zn.t[:, kc, 0:n],
                                  kc == 0, kc == 5, [wq.s, T.zn.s], [ps.s], kc == 5)
                        kb.actf(T.sq.t[:, oc, 0:n], ps.t[:, 0:n], AF.Square, [ps.s], [T.sq.sl[oc]])
                        kb.copy(act if oc % 2 == 0 else dve, T.raw.t[:, oc, 0:n], ps.t[:, 0:n], [ps.s], [T.raw.sl[oc]])
                        yield
                    for j in range(4):
                        qr = next_qo(T)
                        for hh_ in range(2):
                            h = 2 * j + hh_
                            base = hh_ * 64
                            ps = next_ps()
                            kb.mm(ps.t[:, 0:n], ones.t[:, :], T.sq.t[:, h, 0:n], True, False,
                                  [ones.s, T.sq.sl[h]], [ps.s], False)
                            kb.mm(ps.t[:, 0:n], ones.t[base:base + 64, :], T.sq.t[base:base + 64, 8 + j, 0:n],
                                  False, True, [ones.s, T.sq.sl[8 + j]], [ps.s], True)
                            R = T.rh[hh_]
                            rstd_from_ps(dve, R, ps, n, 1.0 / 192, EPS, R.s, ps.s)
                            qn = next_qo(T)
                            kb.stt(dve, qn.t[:, 0:n], T.raw.t[:, h, 0:n], sm.t[:, SP_QHN:SP_QHN + 1], R.t[:, 0:n],
                                   ALU.mult, ALU.mult, [T.raw.sl[h], sm.s, R.s], [qn.s])
                            kb.dma(pool, qT[h * 128:(h + 1) * 128, t0:t0 + n], qn.t[:, 0:n], [qn.s], ())
                            kb.stt(dve, T.rr.t[base:base + 64, 0:n], T.raw.t[base:base + 64, 8 + j, 0:n],
                                   sm.t[base:base + 64, SP_QHR:SP_QHR + 1], R.t[base:base + 64, 0:n],
                                   ALU.mult, ALU.mult, [T.raw.sl[8 + j], sm.s, R.s], [T.rr.s])
                            yield
                        if isx:
                            rope_rot(T, Rc, Rs, T.rr.t, T.rr.s, qr.t, qr.s, n)
                        else:
                            kb.copy(pool, qr.t[:, 0:n], T.rr.t[:, 0:n], [T.rr.s], [qr.s])
                        kb.dma(pool, qT[1024 + j * 128:1024 + (j + 1) * 128, t0:t0 + n], qr.t[:, 0:n], [qr.s], ())
                        yield

                def kv_path(bi):
                    t0, n = BLOCKS[bi]
                    isx = bi > 0
                    b2 = bi % 2
                    T = TK
                    Zkv, Zkr, Rc, Rs = zkv[b2], zkr[b2], rc[b2], rsn[b2]
                    lowrank_norm(T, Zkv, 4, 1.0 / 512, SP_KVN, n)
                    yield
                    for tt in range(n // 128):
                        v = vo[tt % 2]
                        for half in range(2):
                            ps = next_ps()
                            for kc in range(4):
                                kb.mm(ps.t[:, :], T.zn.t[:, kc, tt * 128:(tt + 1) * 128],
                                      wkv.t[:, kc, 1024 + half * 512:1024 + (half + 1) * 512],
                                      kc == 0, kc == 3, [T.zn.s, wkv.s], [ps.s], kc == 3)
                            kb.copy(act if half == 0 else dve, v.t[:, half * 512:(half + 1) * 512], ps.t[:, :],
                                    [ps.s], [v.s])
                        kb.dma(pool, vv[t0 + tt * 128:t0 + (tt + 1) * 128, :], v.t[:, :], [v.s], ())
                        yield
                    kb.actf(T.sq.t[:, 8, 0:n], Zkr.t[:, 0:n], AF.Square, [Zkr.s], [T.sq.sl[8]])
                    kb.ts(dve, T.rr.t[:, 0:n], Zkr.t[:, 0:n], sm.t[:, SP_KHR:SP_KHR + 1], None, ALU.mult, None,
                          [Zkr.s, sm.s], [T.rr.s])
                    if isx:
                        rope_rot(T, Rc, Rs, T.rr.t, T.rr.s, rrot.t, rrot.s, n)
                    else:
                        kb.copy(pool, rrot.t[:, 0:n], T.rr.t[:, 0:n], [T.rr.s], [rrot.s])
                    yield
                    for oc in range(8):
                        ps = next_ps()
                        for kc in range(4):
                            kb.mm(ps.t[:, 0:n], wkv.t[:, kc, oc * 128:(oc + 1) * 128], T.zn.t[:, kc, 0:n],
                                  kc == 0, kc == 3, [wkv.s, T.zn.s], [ps.s], kc == 3)
                        kb.actf(T.sq.t[:, oc, 0:n], ps.t[:, 0:n], AF.Square, [ps.s], [T.sq.sl[oc]])
                        kb.copy(act if oc % 2 == 0 else dve, T.raw.t[:, oc, 0:n], ps.t[:, 0:n], [ps.s], [T.raw.sl[oc]])
                        yield
                    for j in range(4):
                        kr = next_qo(T)
                        for hh_ in range(2):
                            h = 2 * j + hh_
                            base = hh_ * 64
                            ps = next_ps()
                            kb.mm(ps.t[:, 0:n], ones.t[:, :], T.sq.t[:, h, 0:n], True, False,
                                  [ones.s, T.sq.sl[h]], [ps.s], False)
                            kb.mm(ps.t[:, 0:n], ones.t[0:64, :], T.sq.t[0:64, 8, 0:n], False, True,
                                  [ones.s, T.sq.sl[8]], [ps.s], True)
                            R = T.rh[hh_]
                            rstd_from_ps(dve, R, ps, n, 1.0 / 192, EPS, R.s, ps.s)
                            kn = next_qo(T)
                            kb.stt(dve, kn.t[:, 0:n], T.raw.t[:, h, 0:n], sm.t[:, SP_KHN:SP_KHN + 1], R.t[:, 0:n],
                                   ALU.mult, ALU.mult, [T.raw.sl[h], sm.s, R.s], [kn.s])
                            kb.dma(pool, kT[h * 128:(h + 1) * 128, t0:t0 + n], kn.t[:, 0:n], [kn.s], ())
                            kb.tt(pool, kr.t[base:base + 64, 0:n], rrot.t[base:base + 64, 0:n],
                                  R.t[base:base + 64, 0:n], ALU.mult, [rrot.s, R.s], [kr.s])
                            yield
                        kb.dma(pool, kT[1024 + j * 128:1024 + (j + 1) * 128, t0:t0 + n], kr.t[:, 0:n], [kr.s], ())
                        yield

                loads(0)
                for bi in range(len(BLOCKS)):
                    if bi + 1 < len(BLOCKS):
                        loads(bi + 1)
                    isx = bi > 0
                    gens = [kv_path(bi)]
                    if isx or ("c" in seqs):
                        gens.insert(0, q_path(bi))
                    while gens:
                        for g_ in list(gens):
                            try:
                                next(g_)
                            except StopIteration:
                                gens.remove(g_)
                kb.barrier()

        def p2b2_attn(l, seqs):
            scale = 192.0 ** -0.5
            with ExitStack() as st:
                kn = [kb.tile(st, [128, TA], BF16) for _ in range(2)]
                kr = [kb.tile(st, [128, TA], BF16) for _ in range(2)]
                vh = [kb.tile(st, [128, NTT, 128], BF16) for _ in range(2)]
                qn = [kb.tile(st, [128, 512], BF16) for _ in range(2)]
                qr = [kb.tile(st, [128, 512], BF16) for _ in range(2)]
                pt = [kb.tile(st, [128, 512], BF16) for _ in range(4)]
                rcp = [kb.tile(st, [128, 512], F32) for _ in range(2)]
                ob = [kb.tile(st, [128, 512], BF16) for _ in range(2)]
                ps_s = psum[0:4]
                ps_o = psum[4:6]
                ps_l = psum[6:8]
                it = 0
                qi = 0
                for t_ in kr + qr:
                    kb.memset(pool, t_.t[:, :], 0.0, [t_.s])
                for h in range(8):
                    base = (h % 2) * 64
                    j = h // 2
                    K, KR, V = kn[h % 2], kr[h % 2], vh[h % 2]
                    ob_ = 64 - base
                    for t_ in qr:
                        kb.memset(pool, t_.t[ob_:ob_ + 64, :], 0.0, [t_.s])
                    kb.dma(sp, K.t[:, :], kT[h * 128:(h + 1) * 128, :], (), [K.s])
                    kb.dma(sp, KR.t[base:base + 64, :], kT[1024 + j * 128 + base:1024 + j * 128 + base + 64, :],
                           (), [KR.s])
                    vsrc = vv[:, h * 128:(h + 1) * 128].rearrange("(t p) c -> p t c", p=128)
                    for part in range(2):
                        kb.dma(sp, V.t[:, part * 17:(part + 1) * 17, :], vsrc[:, part * 17:(part + 1) * 17, :],
                               (), [V.s])
                    for bi, t0, n in seq_blocks(seqs):
                        Q, QR = qn[qi % 2], qr[qi % 2]
                        po, pl = ps_o[qi % 2], ps_l[qi % 2]
                        rc_, o_ = rcp[qi % 2], ob[qi % 2]
                        qi += 1
                        kb.dma(sp, Q.t[:, 0:n], qT[h * 128:(h + 1) * 128, t0:t0 + n], (), [Q.s])
                        kb.dma(sp, QR.t[base:base + 64, 0:n],
                               qT[1024 + j * 128 + base:1024 + j * 128 + base + 64, t0:t0 + n], (), [QR.s])
                        nkt = 2 if bi == 0 else NTT
                        LOOK = 2
                        stage = {}
                        for i in range(nkt + LOOK):
                            if i < nkt:
                                kt = i
                                ps = ps_s[it % 4]
                                p_ = pt[it % 4]
                                it += 1
                                stage[kt] = p_
                                kb.mm(ps.t[:, 0:n], K.t[:, kt * 128:(kt + 1) * 128], Q.t[:, 0:n], True, False,
                                      [K.s, Q.s], [ps.s], False)
                                kb.mm(ps.t[:, 0:n], KR.t[:, kt * 128:(kt + 1) * 128],
                                      QR.t[:, 0:n], False, True, [KR.s, QR.s], [ps.s], True)
                                kb.actf(p_.t[:, 0:n], ps.t[:, 0:n], AF.Exp, [ps.s], [p_.s], scale=scale)
                            if i >= LOOK:
                                kt = i - LOOK
                                p_ = stage.pop(kt)
                                kb.mm(po.t[:, 0:n], V.t[:, kt, :], p_.t[:, 0:n], kt == 0, kt == nkt - 1,
                                      [V.s, p_.s], [po.s], False)
                                kb.mm(pl.t[:, 0:n], ones.t[:, :], p_.t[:, 0:n], kt == 0, kt == nkt - 1,
                                      [ones.s, p_.s], [pl.s], kt == nkt - 1)
                        kb.op(dve, lambda e, rc_=rc_, pl=pl, n=n: e.reciprocal(out=rc_.t[:, 0:n], in_=pl.t[:, 0:n]),
                              [pl.s], [rc_.s])
                        kb.tt(dve, o_.t[:, 0:n], po.t[:, 0:n], rc_.t[:, 0:n], ALU.mult, [po.s, rc_.s], [o_.s])
                        kb.dma(pool, aT[1, h * 128:(h + 1) * 128, t0:t0 + n], o_.t[:, 0:n], [o_.s], ())
                kb.barrier()

        def p2m_merge(l, seqs, xsrc, xdst):
            with ExitStack() as st:
                sm = load_small(st, l)
                NW = 5
                wr = [kb.tile(st, [128, 8, 1024], BF16) for _ in range(NW)]
                wi = [0]

                def next_w():
                    w = wr[wi[0] % NW]
                    wi[0] += 1
                    return w

                macc = kb.tile(st, [128, 16, 512], F32, nslots=16)
                mbf = kb.tile(st, [128, 16, 512], BF16, nslots=16)
                ain = [kb.tile(st, [128, 8, 512], BF16) for _ in range(2)]
                ysq = kb.tile(st, [128, 8, 512], BF16, nslots=8)
                mean = kb.tile(st, [128, 512], F32)
                rstd = kb.tile(st, [128, 512], F32)
                msq = kb.tile(st, [128, 512], F32)
                ctr = [kb.tile(st, [128, 512], F32) for _ in range(2)]
                gt_ = [kb.tile(st, [128, 4, 512], BF16) for _ in range(2)]
                tmp = [kb.tile(st, [128, 512], F32) for _ in range(3)]
                xio = [kb.tile(st, [128, 1024], F32) for _ in range(3)]
                g1b = kb.tile(st, [128, D], F32)
                gi = [0]
                ti = [0]
                xi = [0]
                cur = [None]
                ains = {}
                wv3 = [w_aout[l, i].rearrange("(kc p) n -> p kc n", p=128) for i in range(3)]
                wvp = w_pool[l].rearrange("(gk p) n -> p gk n", p=128)
                wvo = w_out[l].rearrange("(kc p) n -> p kc n", p=128)

                def ld_branch(br, half):
                    def f():
                        w = next_w()
                        if br == 2:
                            kb.dma(pool, w.t[:, :, 0:512], wvp, (), [w.s])
                        else:
                            widx = {0: 0, 1: 1, 3: 2}[br]
                            kb.dma(pool, w.t[:, :, :], wv3[widx][:, :, half * 1024:(half + 1) * 1024], (), [w.s])
                        return (w,)
                    return f

                def ld_out(ch):
                    def f():
                        wlo_ = next_w()
                        whi_ = next_w()
                        kb.dma(pool, wlo_.t[:, :, :], wvo[:, 0:8, ch * 1024:(ch + 1) * 1024], (), [wlo_.s])
                        kb.dma(pool, whi_.t[:, :, :], wvo[:, 8:16, ch * 1024:(ch + 1) * 1024], (), [whi_.s])
                        return (wlo_, whi_)
                    return f

                def prep_load(bi, t0, n, br):
                    A = ain[br % 2]
                    ains[(bi, br)] = A
                    kb.dma(sp, A.t[:, :, 0:n], aT[br, :, t0:t0 + n].rearrange("(c p) t -> p c t", p=128),
                           (), [A.s])

                def prep_compute(bi, t0, n, br):
                    A = ains[(bi, br)]
                    if br == 3:
                        for c in range(8):
                            kb.actf(ysq.t[:, c, 0:n], A.t[:, c, 0:n], AF.Square, [A.s], [ysq.sl[c]])
                        p1 = next_ps()
                        p2 = next_ps()
                        for c in range(8):
                            kb.mm(p1.t[:, 0:n], ones.t[:, :], A.t[:, c, 0:n], c == 0, c == 7,
                                  [ones.s, A.s], [p1.s], c == 7)
                        for c in range(8):
                            kb.mm(p2.t[:, 0:n], ones.t[:, :], ysq.t[:, c, 0:n], c == 0, c == 7,
                                  [ones.s, ysq.sl[c]], [p2.s], c == 7)
                        kb.ts(dve, mean.t[:, 0:n], p1.t[:, 0:n], 1.0 / 1024, None, ALU.mult, None,
                              [p1.s], [mean.s])
                        kb.tt(dve, msq.t[:, 0:n], mean.t[:, 0:n], mean.t[:, 0:n], ALU.mult, [mean.s], [msq.s])
                        kb.stt(dve, rstd.t[:, 0:n], p2.t[:, 0:n], 1.0 / 1024, msq.t[:, 0:n], ALU.mult,
                               ALU.subtract, [p2.s, msq.s], [rstd.s])
                        kb.actf(rstd.t[:, 0:n], rstd.t[:, 0:n], AF.Sqrt, [rstd.s], [rstd.s],
                                bias=lnepsb.t[:, 0:1], scale=1.0)
                        kb.op(dve, lambda e: e.reciprocal(out=rstd.t[:, 0:n], in_=rstd.t[:, 0:n]),
                              [rstd.s], [rstd.s])
                        for c in range(8):
                            ct = ctr[c % 2]
                            kb.tt(pool, ct.t[:, 0:n], A.t[:, c, 0:n], mean.t[:, 0:n], ALU.subtract,
                                  [A.s, mean.s], [ct.s])
                            kb.tt(dve, ct.t[:, 0:n], ct.t[:, 0:n], rstd.t[:, 0:n], ALU.mult,
                                  [ct.s, rstd.s], [ct.s])
                            kb.actf(A.t[:, c, 0:n], ct.t[:, 0:n], AF.Silu, [ct.s, sm.s], [A.s],
                                    bias=sm.t[:, SP_LNB + c:SP_LNB + c + 1],
                                    scale=sm.t[:, SP_LNG + c:SP_LNG + c + 1])

                def cp_branch(bi, t0, n, br, half):
                    def f(wt):
                        (w,) = wt
                        j = branches.index((bi, t0, n, br))
                        if half == 0:
                            if j == 0:
                                prep_load(bi, t0, n, br)
                                prep_compute(bi, t0, n, br)
                            if j + 1 < len(branches):
                                prep_load(*branches[j + 1])
                        else:
                            if j + 1 < len(branches):
                                prep_compute(*branches[j + 1])
                        A = ains[(bi, br)]
                        for o8 in range(8):
                            oc = half * 8 + o8
                            ps = next_ps()
                            if br == 2:
                                g = oc // 4
                                for k2 in range(2):
                                    kb.mm(ps.t[:, 0:n], w.t[:, g * 2 + k2, (oc % 4) * 128:(oc % 4 + 1) * 128],
                                          A.t[:, g * 2 + k2, 0:n], k2 == 0, k2 == 1, [w.s, A.s], [ps.s], k2 == 1)
                            else:
                                for kc in range(8):
                                    kb.mm(ps.t[:, 0:n], w.t[:, kc, o8 * 128:(o8 + 1) * 128], A.t[:, kc, 0:n],
                                          kc == 0, kc == 7, [w.s, A.s], [ps.s], kc == 7)
                            if oc % 4 == 0:
                                gtile = gt_[gi[0] % 2]
                                gi[0] += 1
                                r0 = ZG + br * D + oc * 128
                                kb.dma(sp, gtile.t[:, :, 0:n],
                                       zx[r0:r0 + 512, t0:t0 + n].rearrange("(c p) t -> p c t", p=128), (),
                                       [gtile.s])
                                cp_branch.gtile = gtile
                            gtile = cp_branch.gtile
                            gta = gtile.t[:, oc % 4, 0:n]
                            if br == 0:
                                kb.tt(dve, macc.t[:, oc, 0:n], ps.t[:, 0:n], gta, ALU.mult,
                                      [ps.s, gtile.s], [macc.sl[oc]])
                            else:
                                tm = tmp[ti[0] % 3]
                                ti[0] += 1
                                if br == 2:
                                    kb.stt(dve, tm.t[:, 0:n], ps.t[:, 0:n], sm.t[:, SP_PS + oc:SP_PS + oc + 1],
                                           gta, ALU.mult, ALU.mult, [ps.s, sm.s, gtile.s], [tm.s])
                                else:
                                    kb.tt(dve, tm.t[:, 0:n], ps.t[:, 0:n], gta, ALU.mult,
                                          [ps.s, gtile.s], [tm.s])
                                if br < 3:
                                    kb.tt(pool, macc.t[:, oc, 0:n], macc.t[:, oc, 0:n], tm.t[:, 0:n], ALU.add,
                                          [macc.sl[oc], tm.s], [macc.sl[oc]])
                                else:
                                    kb.tt(pool, mbf.t[:, oc, 0:n], macc.t[:, oc, 0:n], tm.t[:, 0:n], ALU.add,
                                          [macc.sl[oc], tm.s], [mbf.sl[oc]])
                    return f

                def cp_out(bi, t0, n, ch):
                    def f(wt):
                        wl_, wh_ = wt
                        which = 1 if bi == 0 else 0
                        if which != cur[0]:
                            cur[0] = which
                            kb.dma(sp, g1b.t[:, :], modd[which:which + 1, 2 * D:3 * D].partition_broadcast(128),
                                   (), [g1b.s])
                        for tt in range(n // 128):
                            X = xio[xi[0] % 3]
                            xi[0] += 1
                            kb.dma(sp, X.t[:, :], xsrc[t0 + tt * 128:t0 + (tt + 1) * 128, ch * 1024:(ch + 1) * 1024],
                                   (), [X.s])
                            for cg in range(2):
                                ps = next_ps()
                                for kc in range(16):
                                    w_ = wl_ if kc < 8 else wh_
                                    kb.mm(ps.t[:, :], mbf.t[:, kc, tt * 128:(tt + 1) * 128],
                                          w_.t[:, kc % 8, cg * 512:(cg + 1) * 512], kc == 0, kc == 15,
                                          [mbf.sl[kc], w_.s], [ps.s], kc == 15)
                                c0 = ch * 1024 + cg * 512
                                tm = tmp[ti[0] % 3]
                                ti[0] += 1
                                kb.tt(dve, tm.t[:, :], ps.t[:, :], g1b.t[:, c0:c0 + 512], ALU.mult,
                                      [ps.s, g1b.s], [tm.s])
                                kb.tt(pool, X.t[:, cg * 512:(cg + 1) * 512], X.t[:, cg * 512:(cg + 1) * 512],
                                      tm.t[:, :], ALU.add, [X.s, tm.s], [X.s])
                            kb.dma(act, xdst[t0 + tt * 128:t0 + (tt + 1) * 128, ch * 1024:(ch + 1) * 1024],
                                   X.t[:, :], [X.s], ())
                    return f

                items = []
                branches = [(bi, t0, n, br) for bi, t0, n in seq_blocks(seqs) for br in range(4)]
                for bi, t0, n in seq_blocks(seqs):
                    for br in range(4):
                        if br == 2:
                            items.append((ld_branch(2, 0), cp_branch(bi, t0, n, 2, 0)))
                            items.append((None, cp_branch(bi, t0, n, 2, 1)))
                        else:
                            for half in range(2):
                                items.append((ld_branch(br, half), cp_branch(bi, t0, n, br, half)))
                    for ch in range(2):
                        items.append((ld_out(ch), cp_out(bi, t0, n, ch)))
                PF = 2
                loaded = {}
                last = None
                nxt = 0
                for k, (ldf, cpf) in enumerate(items):
                    while nxt < len(items) and nxt <= k + PF:
                        lf = items[nxt][0]
                        if lf is not None:
                            loaded[nxt] = lf()
                        nxt += 1
                    if ldf is not None:
                        last = loaded.pop(k)
                    cpf(last)
                kb.barrier()

        def p3c_ffconv(l, seqs):
            P = 1
            W = TA + 3 * P
            with ExitStack() as st:
                sm = load_small(st, l)
                grow = [kb.tile(st, [128, W], BF16) for _ in range(2)]
                vrow = [kb.tile(st, [128, TA], BF16) for _ in range(2)]
                orow = [kb.tile(st, [128, TA], BF16, nslots=len(BLOCKS)) for _ in range(2)]
                dgs = [kb.tile(st, [128, 3, 128], BF16) for _ in range(2)]
                tmp = [kb.tile(st, [128, 512], F32) for _ in range(3)]
                for t in grow:
                    kb.memset(pool, t.t[:, :], 0.0, [t.s])
                ta = 0 if "c" in seqs else NCTX
                ti = 0
                for c in range(NFC):
                    g, v, orw, dg = grow[c % 2], vrow[c % 2], orow[c % 2], dgs[c % 2]
                    row_load(sp, g, P, ffT[c * 128:(c + 1) * 128, :], seqs, g.s)
                    kb.dma(sp, v.t[:, ta:TA], ffT[DFF + c * 128:DFF + (c + 1) * 128, ta:TA], (), [v.s])
                    make_diags(dve, dg, sm, SP_CF + c * 3, 3)
                    for bi, t0, n in seq_blocks(seqs):
                        ps = next_ps()
                        p0 = poff(t0, P) - P
                        for k in range(3):
                            kb.mm(ps.t[:, 0:n], dg.t[:, k, :], g.t[:, p0 + k:p0 + k + n], k == 0, k == 2,
                                  [dg.s, g.s], [ps.s], k == 2)
                        tm = tmp[ti % 3]
                        ti += 1
                        kb.actf(tm.t[:, 0:n], ps.t[:, 0:n], AF.Silu, [ps.s, sm.s], [tm.s],
                                bias=sm.t[:, SP_CFB + c:SP_CFB + c + 1])
                        kb.tt(dve, orw.t[:, t0:t0 + n], tm.t[:, 0:n], v.t[:, t0:t0 + n], ALU.mult,
                              [tm.s, v.s], [orw.sl[bi]])
                    kb.dma(pool, fT[c * 128:(c + 1) * 128, ta:TA], orw.t[:, ta:TA],
                           [orw.sl[bi] for bi, _, _ in seq_blocks(seqs)], ())
                kb.barrier()

        def p3d_down(l, seqs, xsrc, xdst, dst_is_out):
            with ExitStack() as st:
                wd = kb.tile(st, [128, NFC, 1024], BF16)
                fin = [kb.tile(st, [128, NFC, 128], BF16) for _ in range(3)]
                xio = [kb.tile(st, [128, 1024], F32) for _ in range(3)]
                tmp = [kb.tile(st, [128, 512], F32) for _ in range(3)]
                g2b = kb.tile(st, [128, D], F32)
                wv = w_down[l].rearrange("(kc p) n -> p kc n", p=128)
                ti = 0
                i = 0
                for ch in range(2):
                    for part in range(4):
                        kb.dma(pool, wd.t[:, part * 11:(part + 1) * 11, :],
                               wv[:, part * 11:(part + 1) * 11, ch * 1024:(ch + 1) * 1024], (), [wd.s])
                    cur = None
                    for tt in range(NTT):
                        which = 1 if tt < 2 else 0
                        if which == 1 and "c" not in seqs:
                            continue
                        if which != cur:
                            cur = which
                            kb.dma(sp, g2b.t[:, :], modd[which:which + 1, 5 * D:6 * D].partition_broadcast(128),
                                   (), [g2b.s])
                        t0 = tt * 128
                        F = fin[i % 3]
                        X = xio[i % 3]
                        i += 1
                        fsrc = fT[:, t0:t0 + 128].rearrange("(c p) t -> p c t", p=128)
                        for part in range(4):
                            kb.dma(sp, F.t[:, part * 11:(part + 1) * 11, :], fsrc[:, part * 11:(part + 1) * 11, :],
                                   (), [F.s])
                        kb.dma(sp, X.t[:, :], xsrc[t0:t0 + 128, ch * 1024:(ch + 1) * 1024], (), [X.s])
                        for cg in range(2):
                            ps = next_ps()
                            for kc in range(NFC):
                                kb.mm(ps.t[:, :], F.t[:, kc, :], wd.t[:, kc, cg * 512:(cg + 1) * 512],
                                      kc == 0, kc == NFC - 1, [F.s, wd.s], [ps.s], kc == NFC - 1)
                            tm = tmp[ti % 3]
                            ti += 1
                            c0 = ch * 1024 + cg * 512
                            kb.tt(dve, tm.t[:, :], ps.t[:, :], g2b.t[:, c0:c0 + 512], ALU.mult, [ps.s, g2b.s], [tm.s])
                            kb.tt(pool, X.t[:, cg * 512:(cg + 1) * 512], X.t[:, cg * 512:(cg + 1) * 512], tm.t[:, :],
                                  ALU.add, [X.s, tm.s], [X.s])
                        if dst_is_out:
                            kb.dma(act, xdst[t0 - NCTX:t0 - NCTX + 128, ch * 1024:(ch + 1) * 1024], X.t[:, :], [X.s], ())
                        else:
                            kb.dma(act, xdst[t0:t0 + 128, ch * 1024:(ch + 1) * 1024], X.t[:, :], [X.s], ())
                kb.barrier()

        def run_all():
            for l in range(n_layers):
                last = (l == DEPTH - 1)
                seqs = "x" if last else "cx"
                xsrc = xin if l == 0 else xl0
                p0_mod(l)
                if stop_after == f"p0_{l}":
                    return
                with ExitStack() as st:
                    hT = make_hT(st, l, xsrc, norm1_g, 0, D, "cx")
                    if stop_after == f"p1a_{l}":
                        return
                    sig = lambda ch: AF.Sigmoid if ch >= ZDG // 128 and ch < ZKR // 128 else None
                    ctx_chunks = set(range(ZKV // 128, ZP // 128)) | {ZKR // 128}
                    proj_fm(l, hT, w_in[l], NIN, zx, seqs, sig, ctx_chunks)
                if stop_after == f"p1_{l}":
                    return
                p2acd_fused(l, seqs)
                if stop_after == f"p2acd_{l}":
                    return
                p2b1_attnprep(l, seqs)
                if stop_after == f"p2b1_{l}":
                    return
                p2b2_attn(l, seqs)
                if stop_after == f"p2b2_{l}":
                    return
                p2m_merge(l, seqs, xsrc, xmid)
                if stop_after == f"p2m_{l}":
                    return
                with ExitStack() as st:
                    hT = make_hT(st, l, xmid, norm2_g, 3 * D, 4 * D, seqs)
                    proj_fm(l, hT, w_up[l], 2 * DFF, ffT, seqs, lambda ch: None)
                if stop_after == f"p3b_{l}":
                    return
                p3c_ffconv(l, seqs)
                if stop_after == f"p3c_{l}":
                    return
                p3d_down(l, seqs, xmid, out if last else xl0, last)

        run_all()
        kb.barrier()
    return nc


def _rope_tables():
    nf = 16
    inv = (10000.0 ** (-np.arange(nf, dtype=np.float32) / nf)).astype(np.float32)
    t = np.arange(NX)
    pos = np.stack([t // 64, t % 64], axis=0).astype(np.float32)
    C = np.zeros((128, NX), np.float32)
    S = np.zeros((128, NX), np.float32)
    for r in range(64):
        axis, f = r // 32, r % 16
        ang = pos[axis] * inv[f]
        C[r] = np.cos(ang)
        S[r] = np.sin(ang)
    C[64:] = C[:64]
    S[64:] = S[:64]
    Pm = np.zeros((128, 128), np.float32)
    for hb in (0, 64):
        for a in range(2):
            for f in range(16):
                r1 = hb + a * 32 + f
                r2 = r1 + 16
                Pm[r1, r2] = -1.0
                Pm[r2, r1] = 1.0
    return C, S, np.ascontiguousarray(Pm.T)


def _prep_shared(inp):
    f = lambda a: np.ascontiguousarray(np.asarray(a, dtype=np.float32))
    perm = np.concatenate([np.arange(0, 4352), np.arange(4416, NIN), np.arange(4352, 4416)])
    w_in = f(np.asarray(inp["w_in"])[:, :, perm])
    hq = np.arange(8)[:, None] * 192
    qperm = np.concatenate([(hq + np.arange(128)[None, :]).ravel(), (hq + 128 + np.arange(64)[None, :]).ravel()])
    w_qup = f(np.asarray(inp["w_q_up"])[:, :, qperm])
    hk = np.arange(8)[:, None] * 256
    kvperm = np.concatenate([(hk + np.arange(128)[None, :]).ravel(), (hk + 128 + np.arange(128)[None, :]).ravel()])
    w_kvup = f(np.asarray(inp["w_kv_up"])[:, :, kvperm])
    w_aout = f(np.stack([inp["w_a_out"], inp["w_mla_out"], inp["w_d_out"]], axis=1))
    w_pool = f(np.asarray(inp["w_pool"]).reshape(DEPTH, 1024, 512))
    sm = np.zeros((DEPTH, 128, NSMALL), np.float32)

    def fm(v, nch):
        return np.asarray(v, np.float32).reshape(nch, 128).T

    for l in range(DEPTH):
        ca = np.asarray(inp["conv_a_w"][l])
        sm[l, :, SP_CA:SP_CA + 24] = ca.T.reshape(8, 128, 3).transpose(1, 0, 2).reshape(128, 24)
        cd = np.asarray(inp["conv_d_w"][l])
        sm[l, :, SP_CD:SP_CD + 248] = cd.T.reshape(8, 128, 31).transpose(1, 0, 2).reshape(128, 248)
        sm[l, :, SP_CDB:SP_CDB + 8] = fm(inp["conv_d_b"][l], 8)
        sm[l, :, SP_LNG:SP_LNG + 8] = fm(inp["cd_ln_g"][l], 8)
        sm[l, :, SP_LNB:SP_LNB + 8] = fm(inp["cd_ln_b"][l], 8)
        sm[l, :, SP_PS:SP_PS + 16] = fm(inp["pool_scale"][l], 16)
        cf = np.asarray(inp["conv_ff_w"][l])
        sm[l, :, SP_CF:SP_CF + 132] = cf.T.reshape(NFC, 128, 3).transpose(1, 0, 2).reshape(128, 132)
        sm[l, :, SP_CFB:SP_CFB + NFC] = fm(inp["conv_ff_b"][l], NFC)
        sm[l, :, SP_QN:SP_QN + 6] = fm(inp["q_norm_g"][l], 6)
        sm[l, :, SP_KVN:SP_KVN + 4] = fm(inp["kv_norm_g"][l], 4)
        qh = np.asarray(inp["q_head_g"][l], np.float32)
        kh = np.asarray(inp["k_head_g"][l], np.float32)
        sm[l, :, SP_QHN] = qh[:128]
        sm[l, :, SP_QHR] = np.concatenate([qh[128:], qh[128:]])
        sm[l, :, SP_KHN] = kh[:128]
        sm[l, :, SP_KHR] = np.concatenate([kh[128:], kh[128:]])
    C, S, PmT = _rope_tables()
    return {
        "ada_w": f(inp["ada_w"]), "ada_b": f(inp["ada_b"]), "norm1_g": f(inp["norm1_g"]),
        "norm2_g": f(inp["norm2_g"]), "w_in": w_in, "w_aout": w_aout, "w_pool": w_pool, "w_out": f(inp["w_out"]),
        "w_qup": w_qup, "w_kvup": w_kvup, "w_up": f(inp["w_up"]), "w_down": f(inp["w_down"]), "small": sm,
        "rope_c": C, "rope_s": S, "pmat": PmT, "ident": np.eye(128, dtype=np.float32),
    }


def _prep_core(inp, b):
    xin = np.concatenate([np.asarray(inp["ctx"][b], np.float32), np.asarray(inp["x"][b], np.float32)], axis=0)
    cc = np.stack([np.asarray(inp["c"][b], np.float32), np.asarray(inp["c_ctx"], np.float32)], axis=-1)
    cc = np.ascontiguousarray(cc.reshape(KC, 128, 2).transpose(1, 0, 2))
    return {"xin": np.ascontiguousarray(xin), "cc": cc}


def kernel(**inputs):
    n = 8
    nc = build()
    shared = _prep_shared(inputs)
    in_maps = []
    for b in range(n):
        m = dict(shared)
        m.update(_prep_core(inputs, b))
        in_maps.append(m)
    res = run_bass_kernel_spmd(nc, in_maps, core_ids=list(range(n)))
    return np.stack([np.asarray(r["out"], dtype=np.float32) for r in res.results], axis=0)
```
